# Optimizing a Trainium2 kernel written in Bass

```python
import math
import jax, jax.numpy as jnp
from jax import lax
import numpy as np

D_MODEL = 1024
BATCH = 32
SEQ = 256
DEPTH = 4
DEC_BATCH = 8
DEC_SEQ = 2048
PAST_LEN = 512

GRID_W = 64
MLA_HEADS = 6
MLA_NOPE = 64
MLA_ROPE = 32
MLA_DV = 64
MLA_KV_RANK = 128
DIFF_HEADS = 6
DIFF_DH = 32
DIFF_DV = 2 * DIFF_DH
FOURIER_GROUPS = 4
FOURIER_GROUP_W = 64
FOURIER_WIDTH = FOURIER_GROUPS * FOURIER_GROUP_W
MIX_WIDTH = MLA_HEADS * MLA_DV + DIFF_HEADS * DIFF_DV + FOURIER_WIDTH
D_FF = 2816
CONV_W = 3
ROPE_BASE = 10000.0
Q_BLOCK = 128
EPS = 1e-6
MLA_SCALE = (MLA_NOPE + MLA_ROPE) ** -0.5
DIFF_SCALE = DIFF_DH ** -0.5
_MLA_Q_END = MLA_HEADS * (MLA_NOPE + MLA_ROPE)
_CKV_END = _MLA_Q_END + MLA_KV_RANK
_KROPE_END = _CKV_END + MLA_ROPE
_DQ_END = _KROPE_END + DIFF_HEADS * 2 * DIFF_DH
_DK_END = _DQ_END + DIFF_HEADS * 2 * DIFF_DH
_DV_END = _DK_END + DIFF_HEADS * DIFF_DV
IN_COLS = _DV_END + FOURIER_WIDTH
IN_SPLITS = (_MLA_Q_END, _CKV_END, _KROPE_END, _DQ_END, _DK_END, _DV_END)

kernel_name = "hybrid_mla_diffattn_fnet_prefix_dit_step"


def _rmsnorm(x, g):
    xf = x.astype(jnp.float32)
    y = xf * lax.rsqrt(jnp.mean(xf * xf, axis=-1, keepdims=True) + EPS)
    return (y * g.astype(jnp.float32)).astype(x.dtype)


def _modulation(cond, w_ada_l, b_ada_l):
    mod = jax.nn.silu(cond) @ w_ada_l + b_ada_l
    mod = mod.reshape(cond.shape[:-1] + (1, 6, D_MODEL))
    return tuple(mod[..., i, :] for i in range(6))


def _modulate(x, g, shift, scale):
    return _rmsnorm(x, g) * (1.0 + scale) + shift


def _axial_rope_tables(n, dim):
    rows = n // GRID_W
    row = jnp.broadcast_to(jnp.arange(rows)[:, None], (rows, GRID_W)).reshape(-1).astype(jnp.float32)
    col = jnp.broadcast_to(jnp.arange(GRID_W)[None, :], (rows, GRID_W)).reshape(-1).astype(jnp.float32)
    quarter = dim // 4
    inv = ROPE_BASE ** (-jnp.arange(quarter, dtype=jnp.float32) / quarter)
    ang_r = row[:, None] * inv
    ang_c = col[:, None] * inv
    return (jnp.cos(ang_r), jnp.sin(ang_r), jnp.cos(ang_c), jnp.sin(ang_c))


def _rotate(x, cos, sin):
    x1, x2 = jnp.split(x, 2, axis=-1)
    return jnp.concatenate([x1 * cos - x2 * sin, x2 * cos + x1 * sin], axis=-1)


def _apply_axial_rope(x, tables):
    mid = [1] * (x.ndim - 3)
    cos_r, sin_r, cos_c, sin_c = (t.reshape(t.shape[0], *mid, t.shape[1]).astype(x.dtype) for t in tables)
    xr, xc = jnp.split(x, 2, axis=-1)
    return jnp.concatenate([_rotate(xr, cos_r, sin_r), _rotate(xc, cos_c, sin_c)], axis=-1)


def _map_query_blocks(fn, qs):
    B, N = qs[0].shape[:2]
    nb = N // Q_BLOCK
    blocks = tuple(q.reshape(B, nb, Q_BLOCK, *q.shape[2:]).swapaxes(0, 1) for q in qs)
    out = lax.map(lambda b: fn(*b), blocks)
    return out.swapaxes(0, 1).reshape(B, N, *out.shape[3:])


def _joint_scores(qb, ks, scale):
    s = jnp.concatenate([jnp.einsum('bqhd,bkhd->bhqk', q, k) for q, k in zip(qb, ks)], axis=-1)
    return s.astype(jnp.float32) * scale


def _joint_attention(qs, ks, v, scale):
    def block(*qb):
        p = jax.nn.softmax(_joint_scores(qb, ks, scale), axis=-1)
        return jnp.einsum('bhqk,bkhd->bqhd', p.astype(v.dtype), v)
    return _map_query_blocks(block, tuple(qs))


def _diff_joint_attention(q1s, k1s, q2s, k2s, v, lam, scale):
    n = len(k1s)
    def block(*qb):
        p1 = jax.nn.softmax(_joint_scores(qb[:n], k1s, scale), axis=-1)
        p2 = jax.nn.softmax(_joint_scores(qb[n:], k2s, scale), axis=-1)
        a = p1 - lam * p2
        return jnp.einsum('bhqk,bkhd->bqhd', a.astype(v.dtype), v)
    return _map_query_blocks(block, tuple(q1s) + tuple(q2s))


def _project(h, w_in_l, g_kv_l):
    B, N = h.shape[:2]
    p = h @ w_in_l
    q_m, ckv, krope, q_d, k_d, v_d, u_f = jnp.split(p, IN_SPLITS, axis=-1)
    return (q_m.reshape(B, N, MLA_HEADS, MLA_NOPE + MLA_ROPE),
            _rmsnorm(ckv, g_kv_l),
            krope,
            q_d.reshape(B, N, DIFF_HEADS, 2 * DIFF_DH),
            k_d.reshape(B, N, DIFF_HEADS, 2 * DIFF_DH),
            v_d.reshape(B, N, DIFF_HEADS, DIFF_DV),
            u_f)


def _mla_kv(ckv, krope, w_uk_l, w_uv_l):
    B, L = ckv.shape[:2]
    k_nope = (ckv @ w_uk_l).reshape(B, L, MLA_HEADS, MLA_NOPE)
    v = (ckv @ w_uv_l).reshape(B, L, MLA_HEADS, MLA_DV)
    k_rope = jnp.broadcast_to(krope[:, :, None, :], (B, L, MLA_HEADS, MLA_ROPE))
    return jnp.concatenate([k_nope, k_rope], axis=-1), v


def _fourier_mix(u):
    B, N = u.shape[:2]
    uf = u.reshape(B, N, FOURIER_GROUPS, FOURIER_GROUP_W).astype(jnp.float32)
    y = jnp.fft.fft2(uf, axes=(1, 3), norm="ortho").real
    return y.reshape(B, N, FOURIER_WIDTH).astype(u.dtype)


def _merge(o_m, o_d, u_f, g_sub_l, lam_init, w_out_l):
    B, N = o_m.shape[:2]
    o_d = _rmsnorm(o_d, g_sub_l) * (1.0 - lam_init)
    o = jnp.concatenate([o_m.reshape(B, N, -1), o_d.reshape(B, N, -1), _fourier_mix(u_f)], axis=-1)
    return o @ w_out_l


def _conv_ffn(h, w_up_l, w_conv_l, w_down_l):
    N = h.shape[1]
    u = h @ w_up_l
    pad = CONV_W // 2
    up = jnp.pad(u, ((0, 0), (pad, pad), (0, 0)))
    u = sum(up[:, j:j + N] * w_conv_l[j] for j in range(CONV_W))
    gate, val = jnp.split(u, 2, axis=-1)
    return (jax.nn.silu(gate) * val) @ w_down_l


def _context_mixer(h, lw, lam, lam_init):
    q_m, ckv, krope, q_d, k_d, v_d, u_f = _project(h, lw['w_in'], lw['g_kv'])
    k_m, v_m = _mla_kv(ckv, krope, lw['w_uk'], lw['w_uv'])
    o_m = _joint_attention((q_m,), (k_m,), v_m, MLA_SCALE)
    o_d = _diff_joint_attention((q_d[..., :DIFF_DH],), (k_d[..., :DIFF_DH],),
                                (q_d[..., DIFF_DH:],), (k_d[..., DIFF_DH:],), v_d, lam, DIFF_SCALE)
    o = _merge(o_m, o_d, u_f, lw['g_sub'], lam_init, lw['w_out'])
    return o, ckv, krope, k_d, v_d


def _latent_mixer(h, ckv_c, krope_c, kd_c, vd_c, lw, lam, lam_init, rope_m, rope_d):
    q_m, ckv, krope, q_d, k_d, v_d, u_f = _project(h, lw['w_in'], lw['g_kv'])
    q_nope, q_rope = jnp.split(q_m, [MLA_NOPE], axis=-1)
    q_lat = jnp.concatenate([q_nope, _apply_axial_rope(q_rope, rope_m)], axis=-1)
    k_lat, v_lat = _mla_kv(ckv, _apply_axial_rope(krope, rope_m), lw['w_uk'], lw['w_uv'])
    k_ctx, v_ctx = _mla_kv(ckv_c, krope_c, lw['w_uk'], lw['w_uv'])
    o_m = _joint_attention((q_lat, q_m), (k_lat, k_ctx),
                           jnp.concatenate([v_lat, v_ctx], axis=1), MLA_SCALE)
    B, N = q_d.shape[:2]
    q_rot = _apply_axial_rope(q_d.reshape(B, N, DIFF_HEADS, 2, DIFF_DH), rope_d).reshape(q_d.shape)
    k_rot = _apply_axial_rope(k_d.reshape(B, N, DIFF_HEADS, 2, DIFF_DH), rope_d).reshape(k_d.shape)
    o_d = _diff_joint_attention(
        (q_rot[..., :DIFF_DH], q_d[..., :DIFF_DH]), (k_rot[..., :DIFF_DH], kd_c[..., :DIFF_DH]),
        (q_rot[..., DIFF_DH:], q_d[..., DIFF_DH:]), (k_rot[..., DIFF_DH:], kd_c[..., DIFF_DH:]),
        jnp.concatenate([v_d, vd_c], axis=1), lam, DIFF_SCALE)
    return _merge(o_m, o_d, u_f, lw['g_sub'], lam_init, lw['w_out'])


def setup_inputs(seed: int = 0) -> dict:
    key = jax.random.key(seed)
    ks = jax.random.split(key, 26)
    f32 = jnp.float32

    def nrm(k, shape, scale=1.0):
        return jax.random.normal(k, shape, f32) * scale

    return {
        "x_prompt": nrm(ks[0], (BATCH, SEQ, D_MODEL)),
        "x_sample": nrm(ks[1], (DEC_BATCH, DEC_SEQ, D_MODEL)),
        "c": nrm(ks[2], (DEC_BATCH, D_MODEL)),
        "cache_mla_ckv": nrm(ks[3], (DEC_BATCH, DEPTH, PAST_LEN, MLA_KV_RANK)),
        "cache_mla_krope": nrm(ks[4], (DEC_BATCH, DEPTH, PAST_LEN, MLA_ROPE)),
        "cache_diff_k": nrm(ks[5], (DEC_BATCH, DEPTH, PAST_LEN, DIFF_HEADS, 2 * DIFF_DH)),
        "cache_diff_v": nrm(ks[6], (DEC_BATCH, DEPTH, PAST_LEN, DIFF_HEADS, DIFF_DV)),
        "c_ctx": nrm(ks[7], (D_MODEL,)),
        "w_ada": nrm(ks[8], (DEPTH, D_MODEL, 6 * D_MODEL), 0.5 * D_MODEL ** -0.5),
        "b_ada": nrm(ks[9], (DEPTH, 6 * D_MODEL), 0.01),
        "g_mix_norm": 1.0 + nrm(ks[10], (DEPTH, D_MODEL), 0.02),
        "g_ffn_norm": 1.0 + nrm(ks[11], (DEPTH, D_MODEL), 0.02),
        "w_in": nrm(ks[12], (DEPTH, D_MODEL, IN_COLS), D_MODEL ** -0.5),
        "g_kv_norm": 1.0 + nrm(ks[13], (DEPTH, MLA_KV_RANK), 0.02),
        "w_uk": nrm(ks[14], (DEPTH, MLA_KV_RANK, MLA_HEADS * MLA_NOPE), MLA_KV_RANK ** -0.5),
        "w_uv": nrm(ks[15], (DEPTH, MLA_KV_RANK, MLA_HEADS * MLA_DV), MLA_KV_RANK ** -0.5),
        "lam_q1": nrm(ks[16], (DEPTH, DIFF_DH), 0.1),
        "lam_k1": nrm(ks[17], (DEPTH, DIFF_DH), 0.1),
        "lam_q2": nrm(ks[18], (DEPTH, DIFF_DH), 0.1),
        "lam_k2": nrm(ks[19], (DEPTH, DIFF_DH), 0.1),
        "g_diff_subln": 1.0 + nrm(ks[20], (DEPTH, DIFF_DV), 0.02),
        "w_out": nrm(ks[21], (DEPTH, MIX_WIDTH, D_MODEL), MIX_WIDTH ** -0.5),
        "w_up": nrm(ks[22], (DEPTH, D_MODEL, 2 * D_FF), D_MODEL ** -0.5),
        "w_conv": nrm(ks[23], (DEPTH, CONV_W, 2 * D_FF), CONV_W ** -0.5),
        "w_down": nrm(ks[24], (DEPTH, D_FF, D_MODEL), D_FF ** -0.5),
        "g_final": 1.0 + nrm(ks[25], (D_MODEL,), 0.02),
    }


def reference(x_prompt, x_sample, c, cache_mla_ckv, cache_mla_krope, cache_diff_k, cache_diff_v,
              c_ctx, w_ada, b_ada, g_mix_norm, g_ffn_norm, w_in, g_kv_norm, w_uk, w_uv,
              lam_q1, lam_k1, lam_q2, lam_k2, g_diff_subln, w_out, w_up, w_conv, w_down, g_final):
    n_lat = x_sample.shape[1]
    rope_m = _axial_rope_tables(n_lat, MLA_ROPE)
    rope_d = _axial_rope_tables(n_lat, DIFF_DH)
    x_c = x_prompt
    x_l = x_sample
    ckv_list, krope_list, dk_list, dv_list = [], [], [], []
    for l in range(DEPTH):
        lw = {'w_in': w_in[l], 'g_kv': g_kv_norm[l], 'w_uk': w_uk[l], 'w_uv': w_uv[l],
              'g_sub': g_diff_subln[l], 'w_out': w_out[l]}
        lam_init = 0.8 - 0.6 * math.exp(-0.3 * l)
        lam = (jnp.exp(jnp.sum(lam_q1[l].astype(jnp.float32) * lam_k1[l].astype(jnp.float32)))
               - jnp.exp(jnp.sum(lam_q2[l].astype(jnp.float32) * lam_k2[l].astype(jnp.float32)))
               + lam_init)

        sa, ca, ga, sf, cf, gf = _modulation(c_ctx, w_ada[l], b_ada[l])
        h = _modulate(x_c, g_mix_norm[l], sa, ca)
        o, ckv, krope, k_d, v_d = _context_mixer(h, lw, lam, lam_init)
        x_c = x_c + ga * o
        h = _modulate(x_c, g_ffn_norm[l], sf, cf)
        x_c = x_c + gf * _conv_ffn(h, w_up[l], w_conv[l], w_down[l])
        ckv_list.append(ckv)
        krope_list.append(krope)
        dk_list.append(k_d)
        dv_list.append(v_d)

        sa, ca, ga, sf, cf, gf = _modulation(c, w_ada[l], b_ada[l])
        h = _modulate(x_l, g_mix_norm[l], sa, ca)
        o = _latent_mixer(h, cache_mla_ckv[:, l], cache_mla_krope[:, l], cache_diff_k[:, l],
                          cache_diff_v[:, l], lw, lam, lam_init, rope_m, rope_d)
        x_l = x_l + ga * o
        h = _modulate(x_l, g_ffn_norm[l], sf, cf)
        x_l = x_l + gf * _conv_ffn(h, w_up[l], w_conv[l], w_down[l])

    y_prompt = _rmsnorm(x_c, g_final)
    y_sample = _rmsnorm(x_l, g_final)
    new_mla_ckv = jnp.stack(ckv_list, axis=1)
    new_mla_krope = jnp.stack(krope_list, axis=1)
    new_diff_k = jnp.stack(dk_list, axis=1)
    new_diff_v = jnp.stack(dv_list, axis=1)
    return (y_prompt, y_sample, new_mla_ckv, new_mla_krope, new_diff_k, new_diff_v)
```

```python
import math
import contextlib
import numpy as np
import concourse.bass as bass
import concourse.mybir as mybir
from concourse.bass_utils import run_bass_kernel_spmd

F32 = mybir.dt.float32
BF16 = mybir.dt.bfloat16
AF = mybir.ActivationFunctionType
ALU = mybir.AluOpType

D = 1024
DEPTH = 4
NCTX = 1024
NLAT = 2048
NTOK = NCTX + NLAT
PAST = 512
EPS = 1e-6
MLA_SCALE = 96 ** -0.5
DIFF_SCALE = 32 ** -0.5
SLABN = 9472


class Op:
    __slots__ = ("eng", "fn", "deps", "sig", "dma", "idx", "sigval", "dsem", "dval", "dprev")

    def __init__(self, eng, fn, sig, dma):
        self.eng = eng
        self.fn = fn
        self.deps = []
        self.sig = sig
        self.dma = dma
        self.sigval = None
        self.dsem = None
        self.dval = None
        self.dprev = 0


class Sched:
    ENGS = ("pe", "act", "dve", "pool", "sp")
    NSLOT = {"sp": 8, "pool": 8, "act": 4, "dve": 1, "pe": 1}

    def __init__(self, nc):
        self.nc = nc
        self.prog = {e: [] for e in self.ENGS}
        self.last_w = {}
        self.readers = {}
        self.overl = {}
        self.nops = 0

    def alias(self, a, b):
        self.overl.setdefault(a, set()).add(b)
        self.overl.setdefault(b, set()).add(a)

    def op(self, eng, fn, r=(), w=(), sig=True, dma=False):
        o = Op(eng, fn, sig, dma)
        deps = set()
        for k in r:
            for kk in (k, *self.overl.get(k, ())):
                lw = self.last_w.get(kk)
                if lw is not None:
                    deps.add(lw)
        for k in w:
            for kk in (k, *self.overl.get(k, ())):
                lw = self.last_w.get(kk)
                if lw is not None:
                    deps.add(lw)
                for rd in self.readers.get(kk, ()):
                    deps.add(rd)
        o.deps = list(deps)
        for k in r:
            self.readers.setdefault(k, []).append(o)
        for k in w:
            self.last_w[k] = o
            self.readers[k] = []
        self.prog[eng].append(o)
        self.nops += 1
        return o

    def dma(self, q, out, in_, r=(), w=()):
        return self.op(q, lambda e: e.dma_start(out=out, in_=in_), r, w, dma=True)

    def emit(self, final_ops=()):
        nc = self.nc
        fin = Op("sp", None, False, False)
        fin.deps = list(final_ops)
        self.prog["sp"].append(fin)
        for e in self.ENGS:
            for o in self.prog[e]:
                for d in o.deps:
                    if d.eng == "pe" and not d.dma and e != "pe":
                        d.sig = True
        for e in self.ENGS:
            comp = [o for o in self.prog[e] if not o.dma and o.fn is not None]
            if comp:
                comp[-1].sig = True
        for e in self.ENGS:
            cnt = 0
            for o in self.prog[e]:
                if o.dma or o.fn is None:
                    continue
                if o.sig:
                    cnt += 1
                    o.sigval = cnt
            nxt = None
            for o in reversed(self.prog[e]):
                if o.dma or o.fn is None:
                    continue
                if o.sig:
                    nxt = o.sigval
                else:
                    o.sigval = nxt
        with contextlib.ExitStack() as st:
            esem = {e: st.enter_context(nc.semaphore("s_" + e)) for e in self.ENGS}
            dsem = {}
            for e in self.ENGS:
                if any(o.dma for o in self.prog[e]):
                    dsem[e] = [st.enter_context(nc.semaphore("d_%s_%d" % (e, i)))
                               for i in range(self.NSLOT[e])]
            for e in self.ENGS:
                k = 0
                uses = {}
                for o in self.prog[e]:
                    if o.dma:
                        s = k % self.NSLOT[e]
                        u = uses.get(s, 0)
                        o.dsem = dsem[e][s]
                        o.dprev = 16 * u
                        o.dval = 16 * (u + 1)
                        uses[s] = u + 1
                        k += 1
            block = st.enter_context(nc.Block())
            engobj = {"pe": block.tensor, "act": block.scalar, "dve": block.vector,
                      "pool": block.gpsimd, "sp": block.sync}
            for e in self.ENGS:
                ops = self.prog[e]
                if not ops:
                    continue

                def body(eng, e=e, ops=ops):
                    waited = {}
                    for o in ops:
                        need = {}
                        for d in o.deps:
                            if d.dma:
                                sv = (d.dsem, d.dval)
                            else:
                                if d.eng == e and e == "pe":
                                    continue
                                sv = (esem[d.eng], d.sigval)
                            key = id(sv[0])
                            if key not in need or need[key][1] < sv[1]:
                                need[key] = sv
                        if o.dma and o.dprev > 0:
                            key = id(o.dsem)
                            if key not in need or need[key][1] < o.dprev:
                                need[key] = (o.dsem, o.dprev)
                        for key, (sem, val) in need.items():
                            if waited.get(key, 0) >= val:
                                continue
                            waited[key] = val
                            eng.wait_ge(sem, val)
                        if o.fn is None:
                            continue
                        ins = o.fn(eng)
                        if o.dma:
                            ins.then_inc(o.dsem, 16)
                        elif o.sig:
                            ins.then_inc(esem[e], 1)

                engobj[e](body)


def _consts():
    c = {}
    t = np.arange(NLAT)
    row = (t // 64).astype(np.float64)
    col = (t % 64).astype(np.float64)
    inv = 10000.0 ** (-np.arange(8, dtype=np.float64) / 8.0)
    cos32 = np.zeros((32, NLAT))
    sin32 = np.zeros((32, NLAT))
    partner = np.zeros(32, dtype=np.int64)
    for d in range(32):
        pos = row if d < 16 else col
        ang = pos * inv[d % 8]
        cos32[d] = np.cos(ang)
        if (d % 16) < 8:
            sin32[d] = -np.sin(ang)
            partner[d] = d + 8
        else:
            sin32[d] = np.sin(ang)
            partner[d] = d - 8
    cosD = np.tile(cos32, (4, 1))
    sinD = np.tile(sin32, (4, 1))
    cosM = np.ones((128, NLAT))
    sinM = np.zeros((128, NLAT))
    cosM[64:96] = cos32
    sinM[64:96] = sin32
    c["rope"] = np.stack([cosM, sinM, cosD, sinD], axis=1).astype(np.float32)
    P32 = np.zeros((32, 32))
    for d in range(32):
        P32[partner[d], d] = 1.0
    PD = np.zeros((128, 128))
    for g in range(4):
        PD[32 * g:32 * g + 32, 32 * g:32 * g + 32] = P32
    PM = np.zeros((128, 128))
    PM[64:96, 64:96] = P32
    ones = np.ones((128, 128))
    bd64 = np.zeros((128, 128))
    bd64[0:64, 0:64] = 1.0
    bd64[64:128, 64:128] = 1.0
    sel = np.zeros((128, 128))
    for d in range(32):
        sel[d, 64 + d] = 1.0
    c["mats"] = np.stack([PM, PD, ones, bd64, sel], axis=1).astype(np.float32)
    dch = np.zeros((128, 2, 512))
    cc = np.arange(64)
    for j in range(2):
        for p in range(128):
            gl, ch = p // 64, p % 64
            g = 2 * j + gl
            ang = 2 * np.pi * ((ch * cc) % 64) / 64.0
            dch[p, j, g * 64:(g + 1) * 64] = np.cos(ang)
            dch[p, j, 256 + g * 64:256 + (g + 1) * 64] = np.sin(ang)
    c["dch"] = dch.astype(np.float32)

    def pos_tab(N):
        n = np.arange(N, dtype=np.int64)
        ang = 2 * np.pi * ((n[:, None] * n[None, :]) % N).astype(np.float64) / N
        s = 1.0 / math.sqrt(64.0 * N)
        return (s * np.cos(ang)), (-s * np.sin(ang))
    TC, TS = pos_tab(256)
    dc = np.zeros((128, 2, 2, 256))
    for b in range(2):
        dc[:, b, 0, :] = TC[b * 128:(b + 1) * 128]
        dc[:, b, 1, :] = TS[b * 128:(b + 1) * 128]
    c["dftc"] = dc.astype(np.float32)
    TC, TS = pos_tab(NLAT)
    dl = np.zeros((4, 2, 128, 8, 2, 512), dtype=np.float32)
    for tt in range(4):
        for hh in range(2):
            for b in range(8):
                n0 = (hh * 8 + b) * 128
                dl[tt, hh, :, b, 0, :] = TC[n0:n0 + 128, tt * 512:(tt + 1) * 512]
                dl[tt, hh, :, b, 1, :] = TS[n0:n0 + 128, tt * 512:(tt + 1) * 512]
    c["dftl"] = dl
    return c


_CONST_CACHE = {}


def build_program(n_layers=DEPTH):
    nc = bass.Bass("TRN2", target_bir_lowering=False, dynamic_dma_scratch_size=4096)

    def din(name, shape):
        return nc.dram_tensor(name, list(shape), F32, kind="ExternalInput").ap()

    def dout(name, shape):
        return nc.dram_tensor(name, list(shape), F32, kind="ExternalOutput").ap()

    xT = din("xT", [128, 8, NTOK])
    condT = din("condT", [128, 8, 2])
    ckvcT = din("ckvcT", [DEPTH, 128, PAST])
    krcT = din("krcT", [DEPTH, 32, PAST])
    kdcT = din("kdcT", [DEPTH, 128, 3, PAST])
    vdc = din("vdc", [DEPTH, 128, 4, 384])
    w_ada = din("w_ada", [DEPTH, D, 6 * D])
    b_adaT = din("b_adaT", [128, DEPTH, 48])
    gvec = din("gvec", [128, DEPTH, 18])
    g_finalT = din("g_finalT", [128, 8])
    w_in = din("w_in", [DEPTH, D, 2144])
    w_uk = din("w_uk", [DEPTH, 128, 384])
    w_uv = din("w_uv", [DEPTH, 128, 384])
    lamv = din("lamv", [128, DEPTH, 4, 32])
    w_out = din("w_out", [DEPTH, D, D])
    w_up = din("w_up", [DEPTH, D, 5632])
    w_convT = din("w_convT", [128, DEPTH, 3, 44])
    w_down = din("w_down", [DEPTH, 2816, D])
    c_rope = din("c_rope", [128, 4, NLAT])
    c_mats = din("c_mats", [128, 5, 128])
    c_dch = din("c_dch", [128, 2, 512])
    c_dftc = din("c_dftc", [128, 2, 2, 256])
    c_dftl = din("c_dftl", [4, 2, 128, 8, 2, 512])

    yT = dout("yT", [128, 8, NTOK])
    ckv_o = dout("ckv_o", [DEPTH, 128, NCTX])
    kr_o = dout("kr_o", [DEPTH, 32, NCTX])
    kd_o = dout("kd_o", [DEPTH, 128, 3, NCTX])
    vd_o = dout("vd_o", [DEPTH, NCTX, 384])
    xs = [nc.dram_tensor("xs%d" % i, [128, 8, NTOK], F32, kind="Internal").ap() for i in range(2)]

    def dbf(name, shape):
        return nc.dram_tensor(name, list(shape), BF16, kind="Internal").ap()

    matsb = dbf("matsb", [128, 5, 128])
    dchb = dbf("dchb", [128, 2, 512])
    ropeb = dbf("ropeb", [128, 4, NLAT])
    wadab = dbf("wadab", [DEPTH, 6, 128, 8, 1024])
    wukb = dbf("wukb", [DEPTH, 128, 384])
    wuvb = dbf("wuvb", [DEPTH, 128, 384])
    winb = dbf("winb", [DEPTH, 2, 128, 8, 1184])
    ckvcb = dbf("ckvcb", [DEPTH, 128, PAST])
    krcb = dbf("krcb", [DEPTH, 32, PAST])
    kdcb = dbf("kdcb", [DEPTH, 128, 3, PAST])
    vdcb = dbf("vdcb", [DEPTH, 128, 4, 384])
    dftcb = dbf("dftcb", [128, 1024])
    dftlb = dbf("dftlb", [4, 2, 128, 8192])
    woutb = dbf("woutb", [DEPTH, 128, 8, 1024])
    wupb = dbf("wupb", [DEPTH, 6, 128, 8, 1024])
    wdnb = dbf("wdnb", [DEPTH, 4, 128, 22, 256])

    st = contextlib.ExitStack()
    with st:
        def sb(name, shape, dt):
            return st.enter_context(nc.sbuf_tensor(name, list(shape), dt))

        S = Sched(nc)
        out_ops = []

        Km = sb("Km", [128, 6, 2560], BF16)
        Kd = sb("Kd", [128, 3, 2560], BF16)
        Vm = sb("Vm", [128, 20, 9, 64], BF16)
        Vd = sb("Vd", [128, 20, 9, 64], BF16)
        AB = sb("AB", [128, 16, 512], BF16)
        slabs = [sb("slab%d" % i, [128, SLABN], BF16) for i in range(2)]
        xw = sb("xw", [128, 8, 512], F32)
        hb = sb("hb", [128, 8, 512], BF16)
        Qm_u = sb("Qm_u", [128, 6, 512], BF16)
        Qm_r = sb("Qm_r", [128, 6, 512], BF16)
        Qd_u = sb("Qd_u", [128, 3, 512], BF16)
        Qd_r = sb("Qd_r", [128, 3, 512], BF16)
        Qp_u = sb("Qp_u", [128, 4, 512], BF16)
        Qp_r = sb("Qp_r", [128, 4, 512], BF16)
        PT = [sb("PT%d" % i, [128, 2, 512], BF16) for i in range(2)]
        oT = hb
        fs = [sb("fs%d" % i, [128, 512], F32) for i in range(4)]
        bs = [sb("bs%d" % i, [128, 512], BF16) for i in range(3)]
        rope = sb("rope", [128, 4, 512], BF16)
        mats = sb("mats", [128, 5, 128], BF16)
        wuk = sb("wuk", [128, 6, 96], BF16)
        wuv = sb("wuv", [128, 384], BF16)
        modsb = sb("modsb", [128, DEPTH, 48, 2], F32)
        gsb = sb("gsb", [128, DEPTH, 2, 8, 2], F32)
        badaT = sb("badaT", [128, DEPTH, 48], F32)
        gv = sb("gv", [128, DEPTH, 18], F32)
        gfin = sb("gfin", [128, 8], F32)
        wcv1 = sb("wcv", [128, 3, 44], F32)
        lamt = sb("lamt", [128, 8], F32)
        neglam = sb("neglam", [128, DEPTH], F32)
        cond_f = sb("cond_f", [128, 8, 2], F32)
        cond_b = sb("cond_b", [128, 8, 2], BF16)
        epsb = sb("epsb", [128, 1], F32)
        act = Km[:, :, :].rearrange("p a b -> p (a b)")[:, 0:22 * 512].rearrange("p (j n) -> p j n", n=512)

        psum = st.enter_context(nc.psum_tensor("psum", [128, 8, 512], F32))

        PM_, PD_, ONES_, BD64_, SEL_ = 0, 1, 2, 3, 4

        rot = {"g": [0, 1], "s": [2, 4], "acc": [6, 7, 0, 1]}
        cnt = {"g": 0, "s": 0, "acc": 0}
        phase = {"ffn": False}

        def bank(cls="g"):
            lst = rot[cls]
            if cls == "g":
                lst = [0, 1, 2, 3, 4, 5, 6, 7]
            b = lst[cnt[cls] % len(lst)]
            cnt[cls] += 1
            return b

        def PS(b):
            return ("ps", b)

        slab_i = [0]

        def next_slab():
            nsl = 4 if phase["ffn"] else 2
            i = slab_i[0] % nsl
            slab_i[0] += 1
            return i, slabs4[i], ("slab", i)

        ev_i = [0]

        def evac(out, in_, r=(), w=(), scale=None, eng=None):
            if eng is None:
                eng = "act" if (ev_i[0] % 2 == 0) else "dve"
                ev_i[0] += 1
            if eng == "act":
                if scale is None:
                    S.op("act", lambda e: e.activation(out=out, in_=in_, func=AF.Copy), r, w)
                else:
                    S.op("act", lambda e: e.activation(out=out, in_=in_, func=AF.Copy, scale=scale), r, w)
            else:
                if scale is None:
                    S.op("dve", lambda e: e.tensor_copy(out=out, in_=in_), r, w)
                else:
                    S.op("dve", lambda e: e.tensor_scalar(out=out, in0=in_, scalar1=scale, scalar2=None,
                                                         op0=ALU.mult), r, w)

        def mm(out, lhsT, rhs, start, stop, r, w, tp=None):
            if tp is None:
                S.op("pe", lambda e: e.matmul(out, lhsT=lhsT, rhs=rhs, start=start, stop=stop), r, w, sig=stop)
            else:
                S.op("pe", lambda e: e.matmul(out, lhsT=lhsT, rhs=rhs, start=start, stop=stop,
                                              tile_position=tp), r, w, sig=stop)


        def ACT(out, in_, func, r=(), w=(), scale=None, bias=None):
            kw = {}
            if scale is not None:
                kw["scale"] = scale
            if bias is not None:
                kw["bias"] = bias
            S.op("act", lambda e: e.activation(out=out, in_=in_, func=func, **kw), r, w)

        def TT(out, in0, in1, op, r=(), w=(), eng="dve"):
            S.op(eng, lambda e: e.tensor_tensor(out=out, in0=in0, in1=in1, op=op), r, w)

        def STT(out, in0, scalar, in1, op0, op1, r=(), w=()):
            S.op("dve", lambda e: e.scalar_tensor_tensor(out=out, in0=in0, scalar=scalar, in1=in1, op0=op0, op1=op1), r, w)

        def TSC(out, in0, scalar1, op0, r=(), w=()):
            S.op("dve", lambda e: e.tensor_scalar(out=out, in0=in0, scalar1=scalar1, scalar2=None, op0=op0), r, w)

        rcp_mode = {"mix": False, "i": 0}

        def RCP(out, in_, r=(), w=()):
            use_act = False
            if rcp_mode["mix"]:
                use_act = (rcp_mode["i"] % 2 == 1)
                rcp_mode["i"] += 1
            if use_act:
                ACT(out, in_, AF.Ln, r=r, w=w)
                ACT(out, out, AF.Exp, r=(), w=[k_ for k_ in w if not (isinstance(k_, tuple) and k_[0] == "ps")], scale=-1.0)
            else:
                S.op("dve", lambda e: e.reciprocal(out=out, in_=in_), r, w)

        def PCPY(out, in_, r=(), w=()):
            S.op("dve", lambda e: e.tensor_copy(out=out, in_=in_), r, w)

        def CPY(out, in_, r=(), w=()):
            S.op("dve", lambda e: e.tensor_copy(out=out, in_=in_), r, w)

        def MSET(ap, val, w=()):
            S.op("dve", lambda e: e.memset(ap, val), (), w)

        def RED(out, in_, r=(), w=()):
            S.op("dve", lambda e: e.tensor_reduce(out=out, in_=in_, axis=mybir.AxisListType.X, op=ALU.add), r, w)


        FGROUPS = [(0, 4), (4, 4), (8, 4), (12, 4), (16, 4), (20, 2)]

        def PC(dst, src, key):
            S.dma("pool", dst, src, w=[key])

        def precast_layer(l, part=None):
            if part in (None, 0):
                precast_layer_a(l)
            if part in (None, 1):
                precast_layer_b(l)

        def precast_layer_a(l):
            wa = w_ada[l].rearrange("(k p) c -> p k c", p=128)
            for s6 in (0, 1):
                PC(wadab[l, s6], wa[:, :, s6 * 1024:(s6 + 1) * 1024], ("wadab", l, s6))
            PC(wukb[l], w_uk[l], ("wukb", l))
            PC(wuvb[l], w_uv[l], ("wuvb", l))
            wsrc = w_in[l].rearrange("(k p) c -> p k c", p=128)
            PC(winb[l, 0][:, :, 0:160], wsrc[:, :, 576:736], ("winb", l, 0))
            PC(winb[l, 0][:, :, 160:1184], wsrc[:, :, 1120:2144], ("winb", l, 0))
            PC(winb[l, 1][:, :, 0:576], wsrc[:, :, 0:576], ("winb", l, 1))
            PC(winb[l, 1][:, :, 576:960], wsrc[:, :, 736:1120], ("winb", l, 1))
            for s6 in (2, 3, 4, 5):
                PC(wadab[l, s6], wa[:, :, s6 * 1024:(s6 + 1) * 1024], ("wadab", l, s6))

        def precast_layer_b(l):
            PC(woutb[l], w_out[l].rearrange("(k p) c -> p k c", p=128), ("woutb", l))
            usrc = w_up[l].rearrange("(k p) c -> p k c", p=128)
            for g, (j0, nj) in enumerate(FGROUPS):
                PC(wupb[l, g][:, :, 0:128 * nj], usrc[:, :, 128 * j0:128 * (j0 + nj)], ("wupb", l, g))
                PC(wupb[l, g][:, :, 512:512 + 128 * nj], usrc[:, :, 2816 + 128 * j0:2816 + 128 * (j0 + nj)], ("wupb", l, g))
            for mp in range(4):
                PC(wdnb[l, mp], w_down[l].rearrange("(j p) c -> p j c", p=128)[:, :, mp * 256:(mp + 1) * 256], ("wdnb", l, mp))

        PC(matsb, c_mats, "matsb")
        PC(dchb, c_dch, "dchb")
        precast_layer(0, part=0)
        PC(ropeb, c_rope, "ropeb")
        PC(dftcb.rearrange("p (b c n) -> p b c n", b=2, c=2), c_dftc, "dftcb")
        precast_layer(0, part=1)
        for l in range(n_layers):
            PC(ckvcb[l], ckvcT[l], ("cache", l))
            PC(krcb[l], krcT[l], ("cache", l))
            PC(kdcb[l], kdcT[l], ("cache", l))
            PC(vdcb[l], vdc[l], ("cache", l))
        for t in range(4):
            for hh in range(2):
                PC(dftlb[t, hh].rearrange("p (b c n) -> p b c n", b=8, c=2), c_dftl[t, hh], ("dftlb", t, hh))
        for l in range(1, n_layers):
            precast_layer(l)

        KM = [("Km", h) for h in range(6)]
        KD = [("Kd", c) for c in range(3)]
        VM = [("Vm", b) for b in range(20)]
        VD = [("Vd", b) for b in range(20)]
        ABK = [("AB", b) for b in range(16)]
        S.dma("sp", mats[:], matsb, r=["matsb"], w=["mats"])
        dch = PT[0]
        S.dma("sp", badaT[:], b_adaT, w=["badaT"])
        S.dma("sp", gv[:], gvec, w=["gv"])
        S.dma("sp", gfin[:], g_finalT, w=["gfin"])
        lamsb = fs[1][:, :].rearrange("p (a b c) -> p a b c", a=DEPTH, b=4)
        S.dma("sp", lamsb, lamv, w=["fs1"])
        S.dma("sp", cond_f[:], condT, w=["cond_f"])
        MSET(Vm[:, :, 1:8:3, :], 1.0, w=VM)
        MSET(Vd[:, :, 1:8:3, :], 1.0, w=VD)
        MSET(wuk[:], 0.0, w=["wuk"])
        MSET(Qp_u[:], 0.0, w=[("Qp_u", i_) for i_ in range(4)])
        MSET(Qp_r[:], 0.0, w=[("Qp_r", i_) for i_ in range(4)])
        MSET(epsb[:], EPS, w=["epsb"])
        for k_ in KM:
            S.alias("act", k_)
        PT.append(rope[:, 0:2, :])
        S.alias("PT2", "rope")
        xw2 = AB[:, :, :].bitcast(F32).rearrange("p a b -> p (a b)").rearrange("p (k n) -> p k n", k=8)
        hb2 = Kd[:, :, :].rearrange("p a b -> p (a b)")[:, 0:4096].rearrange("p (k n) -> p k n", k=8)
        for k_ in range(8):
            for b_ in range(16):
                S.alias(("xw2", k_), ("AB", b_))
            for c_ in range(3):
                S.alias(("hb2", k_), ("Kd", c_))
        BUFS = [(xw, hb, "xw", "hb"), (xw2, hb2, "xw2", "hb2")]
        slabs4 = [slabs[0], slabs[1],
                  Vm[:, :, :, :].rearrange("p a b c -> p (a b c)")[:, 0:SLABN],
                  Vd[:, :, :, :].rearrange("p a b c -> p (a b c)")[:, 0:SLABN]]
        for k_ in VM:
            S.alias(("slab", 2), k_)
        for k_ in VD:
            S.alias(("slab", 3), k_)

        ACT(cond_b[:], cond_f[:], AF.Silu, r=["cond_f"], w=["cond_b"])

        for l in range(n_layers):
            lam_init = 0.8 - 0.6 * math.exp(-0.3 * l)
            for i in range(2):
                TT(fs[0][:, i * 32:(i + 1) * 32], lamsb[:, l, 2 * i, :], lamsb[:, l, 2 * i + 1, :], ALU.mult,
                   r=["fs1"], w=["fs0"])
                RED(lamt[:, i:i + 1], fs[0][:, i * 32:(i + 1) * 32], r=["fs0"], w=["lamt"])
            ACT(lamt[:, 2:4], lamt[:, 0:2], AF.Exp, w=["lamt"])
            STT(neglam[:, l:l + 1], lamt[:, 3:4], -lam_init, lamt[:, 2:3], ALU.add, ALU.subtract,
                r=["lamt"], w=["neglam"])
            TSC(gv[:, l, 17:18], gv[:, l, 17:18], 1.0 - lam_init, ALU.mult, w=["gv"])

        def modulation(l, s6list, norms):
            b = bank("g")
            for s6 in s6list:
                si, slab, skey = next_slab()
                sv = slab[:, 0:8192].rearrange("p (k c) -> p k c", k=8)
                S.dma("sp", sv, wadab[l, s6], r=[("wadab", l, s6)], w=[skey])
                for m in range(8):
                    col = (s6 * 8 + m) * 2
                    for k in range(8):
                        mm(psum[:, b, col:col + 2], sv[:, k, m * 128:(m + 1) * 128], cond_b[:, k, :],
                           k == 0, k == 7, r=[skey, "cond_b"], w=[PS(b)])
            pv = psum[:, b, 0:96].rearrange("p (a c) -> p a c", c=2)
            lo, hi = min(s6list) * 8, (max(s6list) + 1) * 8
            for c in range(2):
                TT(modsb[:, l, lo:hi, c], pv[:, lo:hi, c], badaT[:, l, lo:hi], ALU.add, r=["badaT"], w=[PS(b), "modsb"])
            for nn, (stype, goff) in enumerate(((1, 0), (4, 8))):
                if nn not in norms:
                    continue
                for c in range(2):
                    STT(gsb[:, l, nn, :, c], modsb[:, l, stype * 8:(stype + 1) * 8, c], 1.0, gv[:, l, goff:goff + 8],
                        ALU.add, ALU.mult, r=["modsb", "gv"], w=["gsb"])

        pend_mod = []

        def modap(l, stype, k, c):
            return modsb[:, l, stype * 8 + k, c:c + 1]

        tcount = [0]

        def rstd_from_psum(b, n, inv_count):
            ACT(fs[3][:, 0:n], psum[:, b, 0:n], AF.Ln, w=[PS(b), "fs3"], scale=inv_count, bias=epsb[:, 0:1])
            ACT(fs[3][:, 0:n], fs[3][:, 0:n], AF.Exp, w=["fs3"], scale=-0.5)

        def norm_mod(l, n, nn, c, bs_=0):
            X, H, xk, hk = BUFS[bs_]
            for k in range(8):
                ACT(H[:, k, 0:n], X[:, k, 0:n], AF.Square, r=[(xk, k)], w=[(hk, k)])
            b = bank("g")
            for k in range(8):
                mm(psum[:, b, 0:n], mats[:, ONES_, :], H[:, k, 0:n], k == 0, k == 7, r=["mats", (hk, k)], w=[PS(b)])
            rstd_from_psum(b, n, 1.0 / D)
            for k in range(8):
                ti = tcount[0] % 2
                tcount[0] += 1
                tk = "fs%d" % ti
                TT(fs[ti][:, 0:n], X[:, k, 0:n], fs[3][:, 0:n], ALU.mult, r=[(xk, k), "fs3"], w=[tk])
                ACT(H[:, k, 0:n], fs[ti][:, 0:n], AF.Identity, r=[tk, "gsb", "modsb"], w=[(hk, k)],
                    scale=gsb[:, l, nn, k, c:c + 1], bias=modap(l, 0 if nn == 0 else 3, k, c))

        def load_x(xsrc, xkey, a, n, bs_=0):
            X, H, xk, hk = BUFS[bs_]
            blks = sorted(set([a // 512, (a + n - 1) // 512]))
            for hf in range(2):
                ks = range(4 * hf, 4 * hf + 4)
                S.dma("sp", X[:, 4 * hf:4 * hf + 4, 0:n], xsrc[:, 4 * hf:4 * hf + 4, a:a + n],
                      r=[(xkey, bb, k) for bb in blks for k in ks], w=[(xk, k) for k in ks])

        def store_x(xdst, xdkey, m0, va, vb, r0, r1, bs_=0):
            X, H, xk, hk = BUFS[bs_]
            blks = sorted(set([va // 512, (vb - 1) // 512]))
            ks = range(m0, m0 + 4)
            S.dma("sp", xdst[:, m0:m0 + 4, va:vb], X[:, m0:m0 + 4, r0:r1], r=[(xk, k) for k in ks],
                  w=[(xdkey, bb, k) for bb in blks for k in ks])

        def proj(pout, sv, c0, M, n, skey, b):
            for k in range(8):
                mm(pout, sv[:, k, c0:c0 + M], hb[:, k, 0:n], k == 0, k == 7, r=[skey, ("hb", k)], w=[PS(b)])

        def rms_rows(src_key, src_ap, n, ones_idx, inv_count, b2=None):
            if b2 is None:
                b2 = bank("g")
            ACT(bs[2][:, 0:n], src_ap, AF.Square, r=[src_key], w=["bs2"])
            mm(psum[:, b2, 0:n], mats[:, ones_idx, :], bs[2][:, 0:n], True, True, r=["mats", "bs2"], w=[PS(b2)])
            rstd_from_psum(b2, n, inv_count)

        def rope_apply(dst, dst_keys, src_b, src_key, nrow, perm_idx, ci, tcol0, n):
            b2 = bank("g")
            mm(psum[0:nrow, b2, 0:n], mats[0:nrow, perm_idx, 0:nrow], src_b, True, True, r=["mats", src_key], w=[PS(b2)])
            TT(fs[2][0:nrow, 0:n], psum[0:nrow, b2, 0:n], rope[0:nrow, ci + 1, 0:n], ALU.mult,
               r=["rope"], w=[PS(b2), "fs2"])
            TT(bs[1][0:nrow, 0:n], src_b, rope[0:nrow, ci, 0:n], ALU.mult, r=["rope", src_key], w=["bs1"],
               eng="dve")
            TT(dst, fs[2][0:nrow, 0:n], bs[1][0:nrow, 0:n], ALU.add, r=["fs2", "bs1"], w=dst_keys)

        def vdst(Vb, blk):
            return Vb[:, blk, :, :].rearrange("p (t g) b -> p t g b", g=3)[:, :, 0:3:2, :]

        def v6(ap384):
            return ap384.rearrange("p (t g b) -> p t g b", t=3, g=2)

        def build_km_vm(ckvn, ckvn_key, kr, kr_key, n, kcol0, kblk0):
            for h in range(6):
                b = bank("g")
                mm(psum[0:96, b, 0:n], wuk[:, h, :], ckvn, True, False, r=["wuk", ckvn_key], w=[PS(b)])
                mm(psum[0:96, b, 0:n], mats[0:32, SEL_, 0:96], kr, False, True, r=["mats", kr_key], w=[PS(b)])
                evac(Km[0:96, h, kcol0:kcol0 + n], psum[0:96, b, 0:n], w=[PS(b), ("Km", h)])
            for bi in range(n // 128):
                b = bank("g")
                mm(psum[:, b, 0:384], ckvn[:, bi * 128:(bi + 1) * 128], wuv[:], True, True, r=["wuv", ckvn_key], w=[PS(b)])
                evac(vdst(Vm, kblk0 + bi), v6(psum[:, b, 0:384]), w=[PS(b), ("Vm", kblk0 + bi)])

        stg_i = [0]

        def stg():
            i = stg_i[0] % 2
            stg_i[0] += 1
            return fs[i], "fs%d" % i

        def pass1_tile(l, strm, xsrc, xkey, col0, n, kcol0, kblk0, tcol0, skA):
            sv, skey = skA
            c = strm["cond"]
            lat = strm["rope"]
            load_x(xsrc, xkey, col0, n)
            if lat:
                S.dma("sp", rope[:, :, 0:n], ropeb[:, :, tcol0:tcol0 + n], r=["ropeb"], w=["rope"])
            norm_mod(l, n, 0, c)
            b = bank("g")
            proj(psum[:, b, 0:n], sv, 0, 128, n, skey, b)
            ACT(fs[2][:, 0:n], psum[:, b, 0:n], AF.Copy, w=[PS(b), "fs2"])
            rms_rows("fs2", fs[2][:, 0:n], n, ONES_, 1.0 / 128)
            TT(fs[2][:, 0:n], fs[2][:, 0:n], fs[3][:, 0:n], ALU.mult, r=["fs3"], w=["fs2"])
            ACT(bs[0][:, 0:n], fs[2][:, 0:n], AF.Copy, r=["fs2", "gv"], w=["bs0"], scale=gv[:, l, 16:17])
            if not lat:
                TSC(fs[2][:, 0:n], fs[2][:, 0:n], gv[:, l, 16:17], ALU.mult, r=["gv"], w=["fs2"])
                out_ops.append(S.dma("sp", ckv_o[l][:, col0:col0 + n], fs[2][:, 0:n], r=["fs2"]))
            b = bank("g")
            proj(psum[0:32, b, 0:n], sv, 128, 32, n, skey, b)
            if lat:
                ACT(bs[2][0:32, 0:n], psum[0:32, b, 0:n], AF.Copy, w=[PS(b), "bs2"])
                rope_apply(bs[2][0:32, 0:n], ["bs2"], bs[2][0:32, 0:n], "bs2", 32, PD_, 2, tcol0, n)
            else:
                sg, sgk = stg()
                ACT(sg[0:32, 0:n], psum[0:32, b, 0:n], AF.Copy, w=[PS(b), sgk])
                CPY(bs[2][0:32, 0:n], sg[0:32, 0:n], r=[sgk], w=["bs2"])
                out_ops.append(S.dma("sp", kr_o[l][:, col0:col0 + n], sg[0:32, 0:n], r=[sgk]))
            build_km_vm(bs[0][:, 0:n], "bs0", bs[2][0:32, 0:n], "bs2", n, kcol0, kblk0)
            for cch in range(3):
                b = bank("g")
                proj(psum[:, b, 0:n], sv, 160 + 128 * cch, 128, n, skey, b)
                if lat:
                    ACT(bs[0][:, 0:n], psum[:, b, 0:n], AF.Copy, w=[PS(b), "bs0"])
                    rope_apply(Kd[:, cch, kcol0:kcol0 + n], [("Kd", cch)], bs[0][:, 0:n], "bs0", 128, PD_, 2, tcol0, n)
                else:
                    sg, sgk = stg()
                    ACT(sg[:, 0:n], psum[:, b, 0:n], AF.Copy, w=[PS(b), sgk])
                    CPY(Kd[:, cch, kcol0:kcol0 + n], sg[:, 0:n], r=[sgk], w=[("Kd", cch)])
                    out_ops.append(S.dma("sp", kd_o[l][:, cch, col0:col0 + n], sg[:, 0:n], r=[sgk]))
            for bi in range(n // 128):
                b = bank("g")
                for k in range(8):
                    mm(psum[:, b, 0:384], hb[:, k, bi * 128:(bi + 1) * 128], sv[:, k, 544:928], k == 0, k == 7,
                       r=[skey, ("hb", k)], w=[PS(b)])
                if lat:
                    evac(vdst(Vd, kblk0 + bi), v6(psum[:, b, 0:384]), w=[PS(b), ("Vd", kblk0 + bi)])
                else:
                    sg, sgk = stg()
                    ACT(sg[:, 0:384], psum[:, b, 0:384], AF.Copy, w=[PS(b), sgk])
                    CPY(vdst(Vd, kblk0 + bi), v6(sg[:, 0:384]), r=[sgk], w=[("Vd", kblk0 + bi)])
                    out_ops.append(S.dma("sp", vd_o[l][col0 + bi * 128:col0 + (bi + 1) * 128, :], sg[:, 0:384], r=[sgk]))
            for j in range(2):
                b = bank("g")
                proj(psum[:, b, 0:n], sv, 928 + 128 * j, 128, n, skey, b)
                evac(bs[j][:, 0:n], psum[:, b, 0:n], w=[PS(b), "bs%d" % j])
            for bi in range(n // 128):
                b = bank("g")
                for j in range(2):
                    mm(psum[:, b, 0:512], bs[j][:, bi * 128:(bi + 1) * 128], dch[:, j, :], j == 0, j == 1,
                       r=["PT0", "bs%d" % j], w=[PS(b)])
                evac(AB[:, kblk0 + bi, :], psum[:, b, 0:512], w=[PS(b), ("AB", kblk0 + bi)])

        pending = []

        def flush_pending():
            while pending:
                pending.pop(0)()

        def attn_job(maps, entries, scale):
            accb = [bank("acc"), bank("acc")]
            nkb = len(entries)

            def av(i):
                kb, qc0, nq, rotf, st_, sp_ = entries[i]
                pt = PT[i % 3]
                ptk = "PT%d" % (i % 3)
                for mi, mp in enumerate(maps):
                    mm(psum[:, accb[mi], qc0:qc0 + nq], mp["V"](kb), pt[:, mi, 0:nq], st_, sp_,
                       r=[ptk, (mp["vkey"], kb)], w=[PS(accb[mi])])

            sbs = {}

            def scores(i):
                kb, qc0, nq, rotf, st_, sp_ = entries[i]
                sb0 = bank("s")
                sbs[i] = sb0
                for mi, mp in enumerate(maps):
                    lhsT, tp = mp["K"](kb)
                    mm(psum[:, sb0 + mi, 0:nq], lhsT, mp["Q"](kb, qc0, nq, rotf), True, True, r=mp["r"], w=[PS(sb0 + mi)], tp=tp)

            scores(0)
            for i, (kb, qc0, nq, rotf, st_, sp_) in enumerate(entries):
                if i + 1 < nkb:
                    scores(i + 1)
                sb0 = sbs[i]
                pt = PT[i % 3]
                ptk = "PT%d" % (i % 3)
                ACT(pt[:, :, 0:nq], psum[:, sb0:sb0 + 2, 0:nq], AF.Exp, w=[PS(sb0), PS(sb0 + 1), ptk], scale=scale)
                if i >= 1:
                    av(i - 1)
                if i == min(3, nkb - 1):
                    flush_pending()
            av(nkb - 1)
            return accb

        def vsel(Vb, h, kb):
            hp_ = h // 2
            c0_ = 192 * hp_ + (64 if h % 2 else 0)
            return Vb[:, kb, :, :].rearrange("p a b -> p (a b)")[:, c0_:c0_ + 128]

        def mixer_pass2(l, strm, xsrc, xkey, xdst, xdkey, col0, nq, units, tcol0, skQ, fourier_fn):
            c = strm["cond"]
            lat = strm["rope"]
            sv, skey = skQ
            load_x(xsrc, xkey, col0, nq)
            if lat:
                S.dma("sp", rope[:, :, 0:nq], ropeb[:, :, tcol0:tcol0 + nq], r=["ropeb"], w=["rope"])
            norm_mod(l, nq, 0, c)
            prev_rope = None
            for h in range(6):
                b = bank("g")
                proj(psum[0:96, b, 0:nq], sv, 96 * h, 96, nq, skey, b)
                ACT(Qm_u[0:96, h, 0:nq], psum[0:96, b, 0:nq], AF.Copy, w=[PS(b), ("Qm_u", h)])
                if lat:
                    if prev_rope is not None:
                        rope_apply(*prev_rope)
                    prev_rope = (Qm_r[0:96, h, 0:nq], [("Qm_r", h)], Qm_u[0:96, h, 0:nq], ("Qm_u", h), 96, PM_, 0, tcol0, nq)
            for cch in range(3):
                b = bank("g")
                proj(psum[:, b, 0:nq], sv, 576 + 128 * cch, 128, nq, skey, b)
                ACT(Qd_u[:, cch, 0:nq], psum[:, b, 0:nq], AF.Copy, w=[PS(b), ("Qd_u", cch)])
                if lat:
                    if prev_rope is not None:
                        rope_apply(*prev_rope)
                    prev_rope = (Qd_r[:, cch, 0:nq], [("Qd_r", cch)], Qd_u[:, cch, 0:nq], ("Qd_u", cch), 128, PD_, 2, tcol0, nq)
            if lat and prev_rope is not None:
                rope_apply(*prev_rope)

            rcp_mode["mix"] = not lat
            for entries in units:
                kbs = entries
                qc0, qn = 0, nq
                for m_ in range(2):
                    PCPY(Qp_u[m_ * 32:m_ * 32 + 32, m_, 0:nq], Qd_u[m_ * 32:m_ * 32 + 32, 0, 0:nq], r=[("Qd_u", 0)], w=[("Qp_u", m_)])
                    if lat:
                        PCPY(Qp_r[m_ * 32:m_ * 32 + 32, m_, 0:nq], Qd_r[m_ * 32:m_ * 32 + 32, 0, 0:nq], r=[("Qd_r", 0)], w=[("Qp_r", m_)])
                for hp in range(3):
                    maps = []
                    for h in (2 * hp, 2 * hp + 1):
                        maps.append({
                            "K": (lambda kb, h=h: (Km[0:96, h, kb * 128:(kb + 1) * 128], None)),
                            "Q": (lambda kb, c0, n_, rf, h=h: (Qm_r if rf else Qm_u)[0:96, h, c0:c0 + n_]),
                            "V": (lambda kb, h=h: vsel(Vm, h, kb)),
                            "r": [("Km", h), ("Qm_u", h), ("Qm_r", h)], "vkey": "Vm"})
                    accb = attn_job(maps, kbs, MLA_SCALE)
                    for mi, h in enumerate((2 * hp, 2 * hp + 1)):
                        a = accb[mi]
                        N0 = (h % 2) * 64
                        D0 = 64 - N0
                        RCP(fs[2][D0:D0 + 64, 0:qn], psum[D0:D0 + 64, a, 0:qn], w=[PS(a), "fs2"])
                        TT(oT[N0:N0 + 64, hp, qc0:qc0 + qn], psum[N0:N0 + 64, a, 0:qn],
                           fs[2][D0:D0 + 64, 0:qn], ALU.mult, r=["fs2"], w=[PS(a), ("hb", hp)])
                djobs = [(cch_, hh_) for cch_ in range(3) for hh_ in range(2)]

                def pad_copies(cch_, hh_):
                    for m in range(2):
                        r0 = hh_ * 64 + m * 32
                        slot = hh_ * 2 + m
                        PCPY(Qp_u[r0:r0 + 32, slot, 0:nq], Qd_u[r0:r0 + 32, cch_, 0:nq], r=[("Qd_u", cch_)], w=[("Qp_u", slot)])
                        if lat:
                            PCPY(Qp_r[r0:r0 + 32, slot, 0:nq], Qd_r[r0:r0 + 32, cch_, 0:nq], r=[("Qd_r", cch_)], w=[("Qp_r", slot)])

                for cch in range(3):
                    for hh in range(2):
                        h = 2 * cch + hh
                        ji = 2 * cch + hh
                        if ji + 1 < len(djobs):
                            pad_copies(*djobs[ji + 1])
                        maps = []
                        for m in range(2):
                            slot = hh * 2 + m
                            maps.append({
                                "K": (lambda kb, cch=cch: (Kd[:, cch, kb * 128:(kb + 1) * 128], None)),
                                "Q": (lambda kb, c0, n_, rf, slot=slot: (Qp_r if rf else Qp_u)[:, slot, c0:c0 + n_]),
                                "V": (lambda kb, h=h: vsel(Vd, h, kb)),
                                "r": [("Kd", cch), ("Qp_u", slot), ("Qp_r", slot)], "vkey": "Vd"})
                        accb = attn_job(maps, kbs, DIFF_SCALE)
                        a1, a2 = accb
                        N0 = (h % 2) * 64
                        N1 = N0 + 64
                        D0 = 64 - N0
                        D1 = D0 + 64
                        RCP(fs[2][D0:D1, 0:qn], psum[D0:D1, a1, 0:qn], w=[PS(a1), "fs2"])
                        TT(fs[0][N0:N1, 0:qn], psum[N0:N1, a1, 0:qn], fs[2][D0:D1, 0:qn], ALU.mult, r=["fs2"], w=[PS(a1), "fs0"])
                        RCP(fs[2][D0:D1, 0:qn], psum[D0:D1, a2, 0:qn], w=[PS(a2), "fs2"])
                        TT(fs[2][N0:N1, 0:qn], psum[N0:N1, a2, 0:qn], fs[2][D0:D1, 0:qn], ALU.mult, w=[PS(a2), "fs2"])
                        STT(fs[1][N0:N1, 0:qn], fs[2][N0:N1, 0:qn], neglam[N0:N1, l:l + 1], fs[0][N0:N1, 0:qn],
                            ALU.mult, ALU.add, r=["fs2", "fs0", "neglam"], w=["fs1"])
                    ACT(bs[2][:, 0:qn], fs[1][:, 0:qn], AF.Square, r=["fs1"], w=["bs2"])

                    def post(cch=cch, a2=a2, qc0=qc0, qn=qn):
                        mm(psum[:, a2, 0:qn], mats[:, BD64_, :], bs[2][:, 0:qn], True, True, r=["mats", "bs2"], w=[PS(a2)])
                        rstd_from_psum(a2, qn, 1.0 / 64)
                        TT(fs[1][:, 0:qn], fs[1][:, 0:qn], fs[3][:, 0:qn], ALU.mult, r=["fs3"], w=["fs1"])
                        ACT(oT[:, 3 + cch, qc0:qc0 + qn], fs[1][:, 0:qn], AF.Copy, r=["fs1", "gv"], w=[("hb", 3 + cch)],
                            scale=gv[:, l, 17:18])
                    pending.append(post)
            flush_pending()
            fourier_fn()
            while pend_mod:
                pend_mod.pop(0)()
            si, slab, sk = next_slab()
            svo = slab[:, 0:8192].rearrange("p (k c) -> p k c", k=8)
            S.dma("sp", svo, woutb[l], r=[("woutb", l)], w=[sk])
            for m in range(8):
                b = bank("g")
                for k in range(8):
                    mm(psum[:, b, 0:nq], svo[:, k, m * 128:(m + 1) * 128], oT[:, k, 0:nq], k == 0, k == 7,
                       r=[sk, ("hb", k)], w=[PS(b)])
                STT(xw[:, m, 0:nq], psum[:, b, 0:nq], modap(l, 2, m, c), xw[:, m, 0:nq], ALU.mult, ALU.add,
                    r=["modsb"], w=[PS(b), ("xw", m)])
                if m in (3, 7):
                    store_x(xdst, xdkey, m - 3, col0, col0 + nq, 0, nq)

        def ffn_phase(l, strm, xsrc, xkey, xdst, xdkey, windows, last):
            c = strm["cond"]
            steps = []
            for wi in range(len(windows)):
                for g in range(6):
                    steps.append(("U", wi, g))
                for mp in range(4):
                    steps.append(("D", wi, mp))
            sinfo = {}

            def issue_load(si):
                kind, wi, idx = steps[si]
                _, slab, sk = next_slab()
                if kind == "U":
                    j0, nj = FGROUPS[idx]
                    sv = slab[:, 0:8192].rearrange("p (k c) -> p k c", k=8)
                    if nj == 4:
                        S.dma("sp", sv, wupb[l, idx], r=[("wupb", l, idx)], w=[sk])
                    else:
                        S.dma("sp", sv[:, :, 0:128 * nj], wupb[l, idx][:, :, 0:128 * nj], r=[("wupb", l, idx)], w=[sk])
                        S.dma("sp", sv[:, :, 512:512 + 128 * nj], wupb[l, idx][:, :, 512:512 + 128 * nj],
                              r=[("wupb", l, idx)], w=[sk])
                else:
                    sv = slab[:, 0:22 * 256].rearrange("p (j c) -> p j c", j=22)
                    S.dma("sp", sv, wdnb[l, idx], r=[("wdnb", l, idx)], w=[sk])
                sinfo[si] = (sv, sk)

            nxt = 0
            while nxt < min(4, len(steps)):
                issue_load(nxt)
                nxt += 1
            a0, n0 = windows[0][0], windows[0][1]
            load_x(xsrc, xkey, a0, n0, 0)
            norm_mod(l, n0, 1, c, 0)
            for si, (kind, wi, idx) in enumerate(steps):
                a, n, va, vb, segs = windows[wi]
                bset = wi % 2
                X, H, xk, hk = BUFS[bset]
                sv, sk = sinfo[si]
                if kind == "U":
                    j0, nj = FGROUPS[idx]
                    for jj in range(nj):
                        j = j0 + jj
                        for part in range(2):
                            b = bank("g")
                            ch = j + 22 * part
                            for k in range(8):
                                mm(psum[:, b, 0:n], sv[:, k, part * 512 + jj * 128:part * 512 + (jj + 1) * 128], H[:, k, 0:n],
                                   k == 0, k == 7, r=[sk, (hk, k)], w=[PS(b)])
                            cb = fs[part]
                            ck = "fs%d" % part
                            ACT(cb[:, 0:n], psum[:, b, 0:n], AF.Copy, r=["wcv"], w=[PS(b), ck], scale=wcv1[:, 1, ch:ch + 1])
                            nseg = len(segs)
                            Ls = n // nseg
                            cb3 = cb[:, 0:n].rearrange("p (s m) -> p s m", s=nseg)
                            pu3 = psum[:, b, 0:n].rearrange("p (s m) -> p s m", s=nseg)
                            STT(cb3[:, :, 1:Ls], pu3[:, :, 0:Ls - 1], wcv1[:, 0, ch:ch + 1], cb3[:, :, 1:Ls],
                                ALU.mult, ALU.add, r=["wcv"], w=[PS(b), ck])
                            STT(cb3[:, :, 0:Ls - 1], pu3[:, :, 1:Ls], wcv1[:, 2, ch:ch + 1], cb3[:, :, 0:Ls - 1],
                                ALU.mult, ALU.add, r=["wcv"], w=[PS(b), ck])
                        ACT(fs[2][:, 0:n], fs[0][:, 0:n], AF.Silu, r=["fs0"], w=["fs2"])
                        TT(act[:, j, 0:n], fs[2][:, 0:n], fs[1][:, 0:n], ALU.mult, r=["fs2", "fs1"], w=["act"])
                else:
                    for mm_i in range(2):
                        m = 2 * idx + mm_i
                        b = bank("g")
                        for j in range(22):
                            mm(psum[:, b, 0:n], sv[:, j, mm_i * 128:(mm_i + 1) * 128], act[:, j, 0:n], j == 0, j == 21,
                               r=[sk, "act"], w=[PS(b)])
                        STT(X[:, m, 0:n], psum[:, b, 0:n], modap(l, 5, m, c), X[:, m, 0:n], ALU.mult, ALU.add,
                            r=["modsb"], w=[PS(b), (xk, m)])
                if nxt < len(steps):
                    issue_load(nxt)
                    nxt += 1
                if wi + 1 < len(windows):
                    an, nn_ = windows[wi + 1][0], windows[wi + 1][1]
                    if kind == "U" and idx == 4:
                        load_x(xsrc, xkey, an, nn_, 1 - bset)
                    if kind == "D" and idx == 0:
                        norm_mod(l, nn_, 1, c, 1 - bset)
                if kind == "D" and not last and idx in (1, 3):
                    store_x(xdst, xdkey, (idx // 2) * 4, va, vb, va - a, vb - a, bset)
                if kind == "D" and idx == 3 and last:
                    r0, r1 = va - a, vb - a
                    b = bank("g")
                    for k in range(8):
                        ACT(bs[k % 2][:, 0:n], X[:, k, 0:n], AF.Square, r=[(xk, k)], w=["bs%d" % (k % 2)])
                        mm(psum[:, b, 0:n], mats[:, ONES_, :], bs[k % 2][:, 0:n], k == 0, k == 7, r=["mats", "bs%d" % (k % 2)], w=[PS(b)])
                    rstd_from_psum(b, n, 1.0 / D)
                    for k in range(8):
                        STT(X[:, k, 0:n], X[:, k, 0:n], gfin[:, k:k + 1], fs[3][:, 0:n], ALU.mult, ALU.mult,
                            r=["fs3", "gfin"], w=[(xk, k)])
                    for hf in range(2):
                        out_ops.append(S.dma("sp", yT[:, 4 * hf:4 * hf + 4, va:vb], X[:, 4 * hf:4 * hf + 4, r0:r1],
                                             r=[(xk, k) for k in range(4 * hf, 4 * hf + 4)]))

        STR_C = {"name": "c", "cond": 0, "rope": False}
        STR_L = {"name": "l", "cond": 1, "rope": True}

        def load_win_slab(l, which):
            si, slab, sk = next_slab()
            sv = slab[:, 0:8 * 1184].rearrange("p (k c) -> p k c", k=8)
            if which == "kv":
                S.dma("sp", sv, winb[l, 0], r=[("winb", l, 0)], w=[sk])
            else:
                S.dma("sp", sv[:, :, 0:960], winb[l, 1][:, :, 0:960], r=[("winb", l, 1)], w=[sk])
            return sv, sk

        def make_fourier(lat, t):
            if lat:
                def fourier_fn():
                    accs = [bank("g"), bank("g")]
                    for hh in range(2):
                        si, slab, sk = next_slab()
                        sv = slab[:, 0:8192].rearrange("p (b c n) -> p b c n", b=8, c=2)
                        S.dma("sp", slab[:, 0:8192], dftlb[t, hh], r=[("dftlb", t, hh)], w=[sk])
                        for j in range(2):
                            for bb in range(8):
                                blk = hh * 8 + bb
                                for part in range(2):
                                    first = (hh == 0 and bb == 0 and part == 0)
                                    lastm = (hh == 1 and bb == 7 and part == 1)
                                    mm(psum[:, accs[j], 0:512], AB[:, blk, part * 256 + j * 128:part * 256 + (j + 1) * 128],
                                       sv[:, bb, part, :], first, lastm, r=[("AB", blk), sk], w=[PS(accs[j])])
                    for j in range(2):
                        evac(oT[:, 6 + j, 0:512], psum[:, accs[j], 0:512], w=[PS(accs[j]), ("hb", 6 + j)])
            else:
                def fourier_fn():
                    si, slab, sk = next_slab()
                    sv = slab[:, 0:1024].rearrange("p (b c n) -> p b c n", b=2, c=2)
                    S.dma("sp", slab[:, 0:1024], dftcb, r=["dftcb"], w=[sk])
                    for s in range(2):
                        for j in range(2):
                            b = bank("g")
                            for bb in range(2):
                                blk = 4 * t + 2 * s + bb
                                for part in range(2):
                                    mm(psum[:, b, 0:256], AB[:, blk, part * 256 + j * 128:part * 256 + (j + 1) * 128],
                                       sv[:, bb, part, :], bb == 0 and part == 0, bb == 1 and part == 1,
                                       r=[("AB", blk), sk], w=[PS(b)])
                            evac(oT[:, 6 + j, 256 * s:256 * (s + 1)], psum[:, b, 0:256], w=[PS(b), ("hb", 6 + j)])
            return fourier_fn

        for l in range(n_layers):
            last = (l == n_layers - 1)
            xsrc = xT if l == 0 else xs[0]
            xkey = "xT" if l == 0 else "xs0"
            xmid, xmkey = xs[1], "xs1"
            xdst, xdkey = xs[0], "xs0"
            if l == 0:
                modulation(l, [0, 1], [0])
                pend_mod.append(lambda l=l: modulation(l, [2, 3, 4, 5], [1]))
            else:
                modulation(l, [0, 1, 2, 3, 4, 5], [0, 1])
            S.dma("sp", wcv1[:], w_convT[:, l], w=["wcv"])
            S.dma("sp", wuk[:, :, 0:64], wukb[l].rearrange("p (h c) -> p h c", h=6), r=[("wukb", l)], w=["wuk"])
            S.dma("sp", wuv[:], wuvb[l], r=[("wuvb", l)], w=["wuv"])
            for strm in (STR_C, STR_L):
                lat = strm["rope"]
                phase["ffn"] = False
                base = NCTX if lat else 0
                ntile = 4 if lat else 2
                MSET(Vm[:, :, 1:8:3, :], 1.0, w=VM)
                MSET(Vd[:, :, 1:8:3, :], 1.0, w=VD)
                S.dma("sp", PT[0][:], dchb, r=["dchb"], w=["PT0"])
                skA = load_win_slab(l, "kv")
                for t in range(ntile):
                    pass1_tile(l, strm, xsrc, xkey, base + 512 * t, 512, 512 * t, 4 * t, 512 * t, skA)
                if lat:
                    S.dma("sp", bs[0][:, 0:512], ckvcb[l], r=[("cache", l)], w=["bs0"])
                    S.dma("sp", bs[2][0:32, 0:512], krcb[l], r=[("cache", l)], w=["bs2"])
                    build_km_vm(bs[0][:, 0:512], "bs0", bs[2][0:32, 0:512], "bs2", 512, 2048, 16)
                    S.dma("sp", Kd[:, :, 2048:2560], kdcb[l], r=[("cache", l)], w=KD)
                    for bq in range(4):
                        for t3 in range(3):
                            S.dma("sp", Vd[:, 16 + bq, 3 * t3:3 * t3 + 3:2, :],
                                  vdcb[l][:, bq, 128 * t3:128 * t3 + 128].rearrange("p (g b) -> p g b", g=2),
                                  r=[("cache", l)], w=[("Vd", 16 + bq)])
                for t in range(ntile):
                    col0 = base + 512 * t
                    skQ = load_win_slab(l, "q")
                    if lat:
                        units = [[(kb, 0, 512, kb < 16, kb == 0, kb == 19) for kb in range(20)]]
                    else:
                        units = [[(4 * t + 2 * s + bb, 256 * s, 256, False, bb == 0, bb == 1)
                                  for s in range(2) for bb in range(2)]]
                    mixer_pass2(l, strm, xsrc, xkey, xmid, xmkey, col0, 512, units, 512 * t, skQ, make_fourier(lat, t))
                phase["ffn"] = True
                if lat:
                    bnds = [0, 410, 820, 1230, 1639, 2048]
                    wins = []
                    for wi in range(5):
                        a_ = max(0, bnds[wi] - 1)
                        e_ = min(NLAT, bnds[wi + 1] + 1)
                        wins.append((base + a_, e_ - a_, base + bnds[wi], base + bnds[wi + 1], [(0, e_ - a_)]))
                else:
                    wins = [(512 * t, 512, 512 * t, 512 * t + 512, [(0, 256), (256, 512)]) for t in range(2)]
                ffn_phase(l, strm, xmid, xmkey, xdst, xdkey, wins, last)
        S.emit(out_ops)
        build_program.stats = (S.nops, {e: len(S.prog[e]) for e in S.ENGS})
    return nc


def _prep_inputs(inp):
    if "c" not in _CONST_CACHE:
        _CONST_CACHE["c"] = _consts()
    C = _CONST_CACHE["c"]
    f = lambda a: np.ascontiguousarray(np.asarray(a, dtype=np.float32))
    x_prompt = f(inp["x_prompt"])
    x_sample = f(inp["x_sample"])
    cvec = f(inp["c"])
    c_ctx = f(inp["c_ctx"])

    def fm(v):
        v = np.asarray(v)
        lead = v.shape[:-1]
        vv = v.reshape(lead + (8, 128))
        return np.ascontiguousarray(np.moveaxis(np.moveaxis(vv, -1, 0), -1, 1))

    shared = {}
    shared["w_ada"] = f(inp["w_ada"])
    shared["b_adaT"] = np.ascontiguousarray(f(inp["b_ada"]).reshape(DEPTH, 48, 128).transpose(2, 0, 1))
    gvec = np.zeros((128, DEPTH, 18), np.float32)
    gvec[:, :, 0:8] = f(inp["g_mix_norm"]).reshape(DEPTH, 8, 128).transpose(2, 0, 1)
    gvec[:, :, 8:16] = f(inp["g_ffn_norm"]).reshape(DEPTH, 8, 128).transpose(2, 0, 1)
    gvec[:, :, 16] = f(inp["g_kv_norm"]).T
    shared["gvec_raw_gsub"] = np.ascontiguousarray(np.tile(f(inp["g_diff_subln"]).T, (2, 1)))
    shared["gvec"] = gvec
    shared["g_finalT"] = np.ascontiguousarray(f(inp["g_final"]).reshape(8, 128).T)
    shared["w_in"] = f(inp["w_in"])
    shared["w_uk"] = f(inp["w_uk"])
    shared["w_uv"] = f(inp["w_uv"])
    lam = np.stack([f(inp["lam_q1"]), f(inp["lam_k1"]), f(inp["lam_q2"]), f(inp["lam_k2"])], axis=1)
    shared["lamv"] = np.ascontiguousarray(np.broadcast_to(lam[None], (128, DEPTH, 4, 32)))
    shared["w_out"] = f(inp["w_out"])
    shared["w_up"] = f(inp["w_up"])
    shared["w_convT"] = np.ascontiguousarray(f(inp["w_conv"]).reshape(DEPTH, 3, 44, 128).transpose(3, 0, 1, 2))
    shared["w_down"] = f(inp["w_down"])
    shared["c_rope"] = C["rope"]
    shared["c_mats"] = C["mats"]
    shared["c_dch"] = C["dch"]
    shared["c_dftc"] = C["dftc"]
    shared["c_dftl"] = C["dftl"]
    cm = f(inp["cache_mla_ckv"])
    ck = f(inp["cache_mla_krope"])
    cdk = f(inp["cache_diff_k"])
    cdv = f(inp["cache_diff_v"])
    in_maps = []
    for i in range(8):
        m = dict(shared)
        xt = np.concatenate([x_prompt[4 * i:4 * i + 4].reshape(NCTX, D), x_sample[i]], axis=0)
        m["xT"] = np.ascontiguousarray(xt.reshape(NTOK, 8, 128).transpose(2, 1, 0))
        cond = np.stack([c_ctx, cvec[i]], axis=-1)
        m["condT"] = np.ascontiguousarray(cond.reshape(8, 128, 2).transpose(1, 0, 2))
        m["ckvcT"] = np.ascontiguousarray(cm[i].transpose(0, 2, 1))
        m["krcT"] = np.ascontiguousarray(ck[i].transpose(0, 2, 1))
        kd = cdk[i].reshape(DEPTH, PAST, 3, 128)
        m["kdcT"] = np.ascontiguousarray(kd.transpose(0, 3, 2, 1))
        vd = cdv[i].reshape(DEPTH, 4, 128, 384)
        m["vdc"] = np.ascontiguousarray(vd.transpose(0, 2, 1, 3))
        in_maps.append(m)
    return in_maps


_NC_CACHE = {}


def kernel(**inputs):
    return _run(inputs, DEPTH)


def _run(inputs, n_layers):
    in_maps = _prep_inputs(inputs)
    for m in in_maps:
        gs = m.pop("gvec_raw_gsub")
        gvec = m["gvec"]
        gvec[:, :, 17] = gs
    if n_layers not in _NC_CACHE:
        _NC_CACHE[n_layers] = build_program(n_layers)
    nc = _NC_CACHE[n_layers]
    res = run_bass_kernel_spmd(nc, in_maps, core_ids=list(range(8)))
    R = res.results
    y_prompt = np.zeros((32, 256, D), np.float32)
    y_sample = np.zeros((8, 2048, D), np.float32)
    n_ckv = np.zeros((32, DEPTH, 256, 128), np.float32)
    n_kr = np.zeros((32, DEPTH, 256, 32), np.float32)
    n_dk = np.zeros((32, DEPTH, 256, 6, 64), np.float32)
    n_dv = np.zeros((32, DEPTH, 256, 6, 64), np.float32)
    for i in range(8):
        r = R[i]
        y = np.asarray(r["yT"]).transpose(2, 1, 0).reshape(NTOK, D)
        y_prompt[4 * i:4 * i + 4] = y[:NCTX].reshape(4, 256, D)
        y_sample[i] = y[NCTX:]
        ck = np.asarray(r["ckv_o"])
        n_ckv[4 * i:4 * i + 4] = ck.reshape(DEPTH, 128, 4, 256).transpose(2, 0, 3, 1)
        kr = np.asarray(r["kr_o"])
        n_kr[4 * i:4 * i + 4] = kr.reshape(DEPTH, 32, 4, 256).transpose(2, 0, 3, 1)
        kd = np.asarray(r["kd_o"])
        n_dk[4 * i:4 * i + 4] = kd.reshape(DEPTH, 128, 3, 4, 256).transpose(3, 0, 4, 2, 1).reshape(4, DEPTH, 256, 6, 64)
        vd = np.asarray(r["vd_o"])
        n_dv[4 * i:4 * i + 4] = vd.reshape(DEPTH, 4, 256, 6, 64).transpose(1, 0, 2, 3, 4)
    return (y_prompt, y_sample, n_ckv, n_kr, n_dk, n_dv)
```

```python
import math
import contextlib
import numpy as np
import concourse.bass as bass
import concourse.mybir as mybir
from concourse.bass_utils import run_bass_kernel_spmd

F32 = mybir.dt.float32
BF16 = mybir.dt.bfloat16
AF = mybir.ActivationFunctionType
ALU = mybir.AluOpType

D = 1024
DEPTH = 4
NCTX = 1024
NLAT = 2048
NTOK = NCTX + NLAT
PAST = 512
EPS = 1e-6
MLA_SCALE = 96 ** -0.5
DIFF_SCALE = 32 ** -0.5
SLABN = 9472


class Op:
    __slots__ = ("eng", "fn", "deps", "sig", "dma", "idx", "sigval", "dsem", "dval", "dprev")

    def __init__(self, eng, fn, sig, dma):
        self.eng = eng
        self.fn = fn
        self.deps = []
        self.sig = sig
        self.dma = dma
        self.sigval = None
        self.dsem = None
        self.dval = None
        self.dprev = 0


class Sched:
    ENGS = ("pe", "act", "dve", "pool", "sp")
    NSLOT = {"sp": 8, "pool": 8, "act": 4, "dve": 1, "pe": 1}

    def __init__(self, nc):
        self.nc = nc
        self.prog = {e: [] for e in self.ENGS}
        self.last_w = {}
        self.readers = {}
        self.overl = {}
        self.nops = 0

    def alias(self, a, b):
        self.overl.setdefault(a, set()).add(b)
        self.overl.setdefault(b, set()).add(a)

    def op(self, eng, fn, r=(), w=(), sig=True, dma=False):
        o = Op(eng, fn, sig, dma)
        deps = set()
        for k in r:
            for kk in (k, *self.overl.get(k, ())):
                lw = self.last_w.get(kk)
                if lw is not None:
                    deps.add(lw)
        for k in w:
            for kk in (k, *self.overl.get(k, ())):
                lw = self.last_w.get(kk)
                if lw is not None:
                    deps.add(lw)
                for rd in self.readers.get(kk, ()):
                    deps.add(rd)
        o.deps = list(deps)
        for k in r:
            self.readers.setdefault(k, []).append(o)
        for k in w:
            self.last_w[k] = o
            self.readers[k] = []
        self.prog[eng].append(o)
        self.nops += 1
        return o

    def dma(self, q, out, in_, r=(), w=()):
        return self.op(q, lambda e: e.dma_start(out=out, in_=in_), r, w, dma=True)

    def emit(self, final_ops=()):
        nc = self.nc
        fin = Op("sp", None, False, False)
        fin.deps = list(final_ops)
        self.prog["sp"].append(fin)
        for e in self.ENGS:
            for o in self.prog[e]:
                for d in o.deps:
                    if d.eng == "pe" and not d.dma and e != "pe":
                        d.sig = True
        for e in self.ENGS:
            comp = [o for o in self.prog[e] if not o.dma and o.fn is not None]
            if comp:
                comp[-1].sig = True
        for e in self.ENGS:
            cnt = 0
            for o in self.prog[e]:
                if o.dma or o.fn is None:
                    continue
                if o.sig:
                    cnt += 1
                    o.sigval = cnt
            nxt = None
            for o in reversed(self.prog[e]):
                if o.dma or o.fn is None:
                    continue
                if o.sig:
                    nxt = o.sigval
                else:
                    o.sigval = nxt
        with contextlib.ExitStack() as st:
            esem = {e: st.enter_context(nc.semaphore("s_" + e)) for e in self.ENGS}
            dsem = {}
            for e in self.ENGS:
                if any(o.dma for o in self.prog[e]):
                    dsem[e] = [st.enter_context(nc.semaphore("d_%s_%d" % (e, i)))
                               for i in range(self.NSLOT[e])]
            for e in self.ENGS:
                k = 0
                uses = {}
                for o in self.prog[e]:
                    if o.dma:
                        s = k % self.NSLOT[e]
                        u = uses.get(s, 0)
                        o.dsem = dsem[e][s]
                        o.dprev = 16 * u
                        o.dval = 16 * (u + 1)
                        uses[s] = u + 1
                        k += 1
            block = st.enter_context(nc.Block())
            engobj = {"pe": block.tensor, "act": block.scalar, "dve": block.vector,
                      "pool": block.gpsimd, "sp": block.sync}
            for e in self.ENGS:
                ops = self.prog[e]
                if not ops:
                    continue

                def body(eng, e=e, ops=ops):
                    waited = {}
                    for o in ops:
                        need = {}
                        for d in o.deps:
                            if d.dma:
                                sv = (d.dsem, d.dval)
                            else:
                                if d.eng == e and e == "pe":
                                    continue
                                sv = (esem[d.eng], d.sigval)
                            key = id(sv[0])
                            if key not in need or need[key][1] < sv[1]:
                                need[key] = sv
                        if o.dma and o.dprev > 0:
                            key = id(o.dsem)
                            if key not in need or need[key][1] < o.dprev:
                                need[key] = (o.dsem, o.dprev)
                        for key, (sem, val) in need.items():
                            if waited.get(key, 0) >= val:
                                continue
                            waited[key] = val
                            eng.wait_ge(sem, val)
                        if o.fn is None:
                            continue
                        ins = o.fn(eng)
                        if o.dma:
                            ins.then_inc(o.dsem, 16)
                        elif o.sig:
                            ins.then_inc(esem[e], 1)

                engobj[e](body)


def _consts():
    c = {}
    t = np.arange(NLAT)
    row = (t // 64).astype(np.float64)
    col = (t % 64).astype(np.float64)
    inv = 10000.0 ** (-np.arange(8, dtype=np.float64) / 8.0)
    cos32 = np.zeros((32, NLAT))
    sin32 = np.zeros((32, NLAT))
    partner = np.zeros(32, dtype=np.int64)
    for d in range(32):
        pos = row if d < 16 else col
        ang = pos * inv[d % 8]
        cos32[d] = np.cos(ang)
        if (d % 16) < 8:
            sin32[d] = -np.sin(ang)
            partner[d] = d + 8
        else:
            sin32[d] = np.sin(ang)
            partner[d] = d - 8
    cosD = np.tile(cos32, (4, 1))
    sinD = np.tile(sin32, (4, 1))
    cosM = np.ones((128, NLAT))
    sinM = np.zeros((128, NLAT))
    cosM[64:96] = cos32
    sinM[64:96] = sin32
    c["rope"] = np.stack([cosM, sinM, cosD, sinD], axis=1).astype(np.float32)
    P32 = np.zeros((32, 32))
    for d in range(32):
        P32[partner[d], d] = 1.0
    PD = np.zeros((128, 128))
    for g in range(4):
        PD[32 * g:32 * g + 32, 32 * g:32 * g + 32] = P32
    PM = np.zeros((128, 128))
    PM[64:96, 64:96] = P32
    ones = np.ones((128, 128))
    bd64 = np.zeros((128, 128))
    bd64[0:64, 0:64] = 1.0
    bd64[64:128, 64:128] = 1.0
    sel = np.zeros((128, 128))
    for d in range(32):
        sel[d, 64 + d] = 1.0
    c["mats"] = np.stack([PM, PD, ones, bd64, sel], axis=1).astype(np.float32)
    dch = np.zeros((128, 2, 512))
    cc = np.arange(64)
    for j in range(2):
        for p in range(128):
            gl, ch = p // 64, p % 64
            g = 2 * j + gl
            ang = 2 * np.pi * ((ch * cc) % 64) / 64.0
            dch[p, j, g * 64:(g + 1) * 64] = np.cos(ang)
            dch[p, j, 256 + g * 64:256 + (g + 1) * 64] = np.sin(ang)
    c["dch"] = dch.astype(np.float32)

    def pos_tab(N):
        n = np.arange(N, dtype=np.int64)
        ang = 2 * np.pi * ((n[:, None] * n[None, :]) % N).astype(np.float64) / N
        s = 1.0 / math.sqrt(64.0 * N)
        return (s * np.cos(ang)), (-s * np.sin(ang))
    TC, TS = pos_tab(256)
    dc = np.zeros((128, 2, 2, 256))
    for b in range(2):
        dc[:, b, 0, :] = TC[b * 128:(b + 1) * 128]
        dc[:, b, 1, :] = TS[b * 128:(b + 1) * 128]
    c["dftc"] = dc.astype(np.float32)
    TC, TS = pos_tab(NLAT)
    dl = np.zeros((4, 2, 128, 8, 2, 512), dtype=np.float32)
    for tt in range(4):
        for hh in range(2):
            for b in range(8):
                n0 = (hh * 8 + b) * 128
                dl[tt, hh, :, b, 0, :] = TC[n0:n0 + 128, tt * 512:(tt + 1) * 512]
                dl[tt, hh, :, b, 1, :] = TS[n0:n0 + 128, tt * 512:(tt + 1) * 512]
    c["dftl"] = dl
    return c


_CONST_CACHE = {}


def build_program(n_layers=DEPTH):
    nc = bass.Bass("TRN2", target_bir_lowering=False, dynamic_dma_scratch_size=4096)

    def din(name, shape):
        return nc.dram_tensor(name, list(shape), F32, kind="ExternalInput").ap()

    def dout(name, shape):
        return nc.dram_tensor(name, list(shape), F32, kind="ExternalOutput").ap()

    xT = din("xT", [128, 8, NTOK])
    condT = din("condT", [128, 8, 2])
    ckvcT = din("ckvcT", [DEPTH, 128, PAST])
    krcT = din("krcT", [DEPTH, 32, PAST])
    kdcT = din("kdcT", [DEPTH, 128, 3, PAST])
    vdc = din("vdc", [DEPTH, 128, 4, 384])
    w_ada = din("w_ada", [DEPTH, D, 6 * D])
    b_adaT = din("b_adaT", [128, DEPTH, 48])
    gvec = din("gvec", [128, DEPTH, 18])
    g_finalT = din("g_finalT", [128, 8])
    w_in = din("w_in", [DEPTH, D, 2144])
    w_uk = din("w_uk", [DEPTH, 128, 384])
    w_uv = din("w_uv", [DEPTH, 128, 384])
    lamv = din("lamv", [128, DEPTH, 4, 32])
    w_out = din("w_out", [DEPTH, D, D])
    w_up = din("w_up", [DEPTH, D, 5632])
    w_convT = din("w_convT", [128, DEPTH, 3, 44])
    w_down = din("w_down", [DEPTH, 2816, D])
    c_rope = din("c_rope", [128, 4, NLAT])
    c_mats = din("c_mats", [128, 5, 128])
    c_dch = din("c_dch", [128, 2, 512])
    c_dftc = din("c_dftc", [128, 2, 2, 256])
    c_dftl = din("c_dftl", [4, 2, 128, 8, 2, 512])

    yT = dout("yT", [128, 8, NTOK])
    ckv_o = dout("ckv_o", [DEPTH, 128, NCTX])
    kr_o = dout("kr_o", [DEPTH, 32, NCTX])
    kd_o = dout("kd_o", [DEPTH, 128, 3, NCTX])
    vd_o = dout("vd_o", [DEPTH, NCTX, 384])
    xs = [nc.dram_tensor("xs%d" % i, [128, 8, NTOK], F32, kind="Internal").ap() for i in range(2)]

    def dbf(name, shape):
        return nc.dram_tensor(name, list(shape), BF16, kind="Internal").ap()

    matsb = dbf("matsb", [128, 5, 128])
    dchb = dbf("dchb", [128, 2, 512])
    ropeb = dbf("ropeb", [128, 4, NLAT])
    wadab = dbf("wadab", [DEPTH, 6, 128, 8, 1024])
    wukb = dbf("wukb", [DEPTH, 128, 384])
    wuvb = dbf("wuvb", [DEPTH, 128, 384])
    winb = dbf("winb", [DEPTH, 2, 128, 8, 1184])
    ckvcb = dbf("ckvcb", [DEPTH, 128, PAST])
    krcb = dbf("krcb", [DEPTH, 32, PAST])
    kdcb = dbf("kdcb", [DEPTH, 128, 3, PAST])
    vdcb = dbf("vdcb", [DEPTH, 128, 4, 384])
    dftcb = dbf("dftcb", [128, 1024])
    dftlb = dbf("dftlb", [4, 2, 128, 8192])
    woutb = dbf("woutb", [DEPTH, 128, 8, 1024])
    wupb = dbf("wupb", [DEPTH, 6, 128, 8, 1024])
    wdnb = dbf("wdnb", [DEPTH, 4, 128, 22, 256])

    st = contextlib.ExitStack()
    with st:
        def sb(name, shape, dt):
            return st.enter_context(nc.sbuf_tensor(name, list(shape), dt))

        S = Sched(nc)
        out_ops = []

        Km = sb("Km", [128, 6, 2560], BF16)
        Kd = sb("Kd", [128, 3, 2560], BF16)
        Vm = sb("Vm", [128, 20, 9, 64], BF16)
        Vd = sb("Vd", [128, 20, 9, 64], BF16)
        AB = sb("AB", [128, 16, 512], BF16)
        slabs = [sb("slab%d" % i, [128, SLABN], BF16) for i in range(2)]
        xw = sb("xw", [128, 8, 512], F32)
        hb = sb("hb", [128, 8, 512], BF16)
        Qm_u = sb("Qm_u", [128, 6, 512], BF16)
        Qm_r = sb("Qm_r", [128, 6, 512], BF16)
        Qd_u = sb("Qd_u", [128, 3, 512], BF16)
        Qd_r = sb("Qd_r", [128, 3, 512], BF16)
        Qp_u = sb("Qp_u", [128, 4, 512], BF16)
        Qp_r = sb("Qp_r", [128, 4, 512], BF16)
        PT = [sb("PT%d" % i, [128, 2, 512], BF16) for i in range(2)]
        oT = hb
        fs = [sb("fs%d" % i, [128, 512], F32) for i in range(4)]
        bs = [sb("bs%d" % i, [128, 512], BF16) for i in range(3)]
        rope = sb("rope", [128, 4, 512], BF16)
        mats = sb("mats", [128, 5, 128], BF16)
        wuk = sb("wuk", [128, 6, 96], BF16)
        wuv = sb("wuv", [128, 384], BF16)
        modsb = sb("modsb", [128, DEPTH, 48, 2], F32)
        gsb = sb("gsb", [128, DEPTH, 2, 8, 2], F32)
        badaT = sb("badaT", [128, DEPTH, 48], F32)
        gv = sb("gv", [128, DEPTH, 18], F32)
        gfin = sb("gfin", [128, 8], F32)
        wcv1 = sb("wcv", [128, 3, 44], F32)
        lamt = sb("lamt", [128, 8], F32)
        neglam = sb("neglam", [128, DEPTH], F32)
        cond_f = sb("cond_f", [128, 8, 2], F32)
        cond_b = sb("cond_b", [128, 8, 2], BF16)
        epsb = sb("epsb", [128, 1], F32)
        act = Km[:, :, :].rearrange("p a b -> p (a b)")[:, 0:22 * 512].rearrange("p (j n) -> p j n", n=512)

        psum = st.enter_context(nc.psum_tensor("psum", [128, 8, 512], F32))

        PM_, PD_, ONES_, BD64_, SEL_ = 0, 1, 2, 3, 4

        rot = {"g": [0, 1], "s": [2, 4], "acc": [6, 7, 0, 1]}
        cnt = {"g": 0, "s": 0, "acc": 0}
        phase = {"ffn": False}

        def bank(cls="g"):
            lst = rot[cls]
            if cls == "g":
                lst = [0, 1, 2, 3, 4, 5, 6, 7]
            b = lst[cnt[cls] % len(lst)]
            cnt[cls] += 1
            return b

        def PS(b):
            return ("ps", b)

        slab_i = [0]

        def next_slab():
            nsl = 4 if phase["ffn"] else 2
            i = slab_i[0] % nsl
            slab_i[0] += 1
            return i, slabs4[i], ("slab", i)

        ev_i = [0]

        def evac(out, in_, r=(), w=(), scale=None, eng=None):
            if eng is None:
                eng = "act" if (ev_i[0] % 2 == 0) else "dve"
                ev_i[0] += 1
            if eng == "act":
                if scale is None:
                    S.op("act", lambda e: e.activation(out=out, in_=in_, func=AF.Copy), r, w)
                else:
                    S.op("act", lambda e: e.activation(out=out, in_=in_, func=AF.Copy, scale=scale), r, w)
            else:
                if scale is None:
                    S.op("dve", lambda e: e.tensor_copy(out=out, in_=in_), r, w)
                else:
                    S.op("dve", lambda e: e.tensor_scalar(out=out, in0=in_, scalar1=scale, scalar2=None,
                                                         op0=ALU.mult), r, w)

        def mm(out, lhsT, rhs, start, stop, r, w, tp=None):
            if tp is None:
                S.op("pe", lambda e: e.matmul(out, lhsT=lhsT, rhs=rhs, start=start, stop=stop), r, w, sig=stop)
            else:
                S.op("pe", lambda e: e.matmul(out, lhsT=lhsT, rhs=rhs, start=start, stop=stop,
                                              tile_position=tp), r, w, sig=stop)


        def ACT(out, in_, func, r=(), w=(), scale=None, bias=None):
            kw = {}
            if scale is not None:
                kw["scale"] = scale
            if bias is not None:
                kw["bias"] = bias
            S.op("act", lambda e: e.activation(out=out, in_=in_, func=func, **kw), r, w)

        def TT(out, in0, in1, op, r=(), w=(), eng="dve"):
            S.op(eng, lambda e: e.tensor_tensor(out=out, in0=in0, in1=in1, op=op), r, w)

        def STT(out, in0, scalar, in1, op0, op1, r=(), w=()):
            S.op("dve", lambda e: e.scalar_tensor_tensor(out=out, in0=in0, scalar=scalar, in1=in1, op0=op0, op1=op1), r, w)

        def TSC(out, in0, scalar1, op0, r=(), w=()):
            S.op("dve", lambda e: e.tensor_scalar(out=out, in0=in0, scalar1=scalar1, scalar2=None, op0=op0), r, w)

        rcp_mode = {"mix": False, "i": 0}

        def RCP(out, in_, r=(), w=()):
            use_act = False
            if rcp_mode["mix"]:
                use_act = (rcp_mode["i"] % 2 == 1)
                rcp_mode["i"] += 1
            if use_act:
                ACT(out, in_, AF.Ln, r=r, w=w)
                ACT(out, out, AF.Exp, r=(), w=[k_ for k_ in w if not (isinstance(k_, tuple) and k_[0] == "ps")], scale=-1.0)
            else:
                S.op("dve", lambda e: e.reciprocal(out=out, in_=in_), r, w)

        def PCPY(out, in_, r=(), w=()):
            S.op("dve", lambda e: e.tensor_copy(out=out, in_=in_), r, w)

        def CPY(out, in_, r=(), w=()):
            S.op("dve", lambda e: e.tensor_copy(out=out, in_=in_), r, w)

        def MSET(ap, val, w=()):
            S.op("dve", lambda e: e.memset(ap, val), (), w)

        def RED(out, in_, r=(), w=()):
            S.op("dve", lambda e: e.tensor_reduce(out=out, in_=in_, axis=mybir.AxisListType.X, op=ALU.add), r, w)


        FGROUPS = [(0, 4), (4, 4), (8, 4), (12, 4), (16, 4), (20, 2)]

        def PC(dst, src, key):
            S.dma("pool", dst, src, w=[key])

        def precast_layer(l, part=None):
            if part in (None, 0):
                precast_layer_a(l)
            if part in (None, 1):
                precast_layer_b(l)

        def precast_layer_a(l):
            wa = w_ada[l].rearrange("(k p) c -> p k c", p=128)
            for s6 in (0, 1):
                PC(wadab[l, s6], wa[:, :, s6 * 1024:(s6 + 1) * 1024], ("wadab", l, s6))
            PC(wukb[l], w_uk[l], ("wukb", l))
            PC(wuvb[l], w_uv[l], ("wuvb", l))
            wsrc = w_in[l].rearrange("(k p) c -> p k c", p=128)
            PC(winb[l, 0][:, :, 0:160], wsrc[:, :, 576:736], ("winb", l, 0))
            PC(winb[l, 0][:, :, 160:1184], wsrc[:, :, 1120:2144], ("winb", l, 0))
            PC(winb[l, 1][:, :, 0:576], wsrc[:, :, 0:576], ("winb", l, 1))
            PC(winb[l, 1][:, :, 576:960], wsrc[:, :, 736:1120], ("winb", l, 1))
            for s6 in (2, 3, 4, 5):
                PC(wadab[l, s6], wa[:, :, s6 * 1024:(s6 + 1) * 1024], ("wadab", l, s6))

        def precast_layer_b(l):
            PC(woutb[l], w_out[l].rearrange("(k p) c -> p k c", p=128), ("woutb", l))
            usrc = w_up[l].rearrange("(k p) c -> p k c", p=128)
            for g, (j0, nj) in enumerate(FGROUPS):
                PC(wupb[l, g][:, :, 0:128 * nj], usrc[:, :, 128 * j0:128 * (j0 + nj)], ("wupb", l, g))
                PC(wupb[l, g][:, :, 512:512 + 128 * nj], usrc[:, :, 2816 + 128 * j0:2816 + 128 * (j0 + nj)], ("wupb", l, g))
            for mp in range(4):
                PC(wdnb[l, mp], w_down[l].rearrange("(j p) c -> p j c", p=128)[:, :, mp * 256:(mp + 1) * 256], ("wdnb", l, mp))

        PC(matsb, c_mats, "matsb")
        PC(dchb, c_dch, "dchb")
        precast_layer(0, part=0)
        PC(ropeb, c_rope, "ropeb")
        PC(dftcb.rearrange("p (b c n) -> p b c n", b=2, c=2), c_dftc, "dftcb")
        precast_layer(0, part=1)
        for l in range(n_layers):
            PC(ckvcb[l], ckvcT[l], ("cache", l))
            PC(krcb[l], krcT[l], ("cache", l))
            PC(kdcb[l], kdcT[l], ("cache", l))
            PC(vdcb[l], vdc[l], ("cache", l))
        for t in range(4):
            for hh in range(2):
                PC(dftlb[t, hh].rearrange("p (b c n) -> p b c n", b=8, c=2), c_dftl[t, hh], ("dftlb", t, hh))
        for l in range(1, n_layers):
            precast_layer(l)

        KM = [("Km", h) for h in range(6)]
        KD = [("Kd", c) for c in range(3)]
        VM = [("Vm", b) for b in range(20)]
        VD = [("Vd", b) for b in range(20)]
        ABK = [("AB", b) for b in range(16)]
        S.dma("sp", mats[:], matsb, r=["matsb"], w=["mats"])
        dch = PT[0]
        S.dma("sp", badaT[:], b_adaT, w=["badaT"])
        S.dma("sp", gv[:], gvec, w=["gv"])
        S.dma("sp", gfin[:], g_finalT, w=["gfin"])
        lamsb = fs[1][:, :].rearrange("p (a b c) -> p a b c", a=DEPTH, b=4)
        S.dma("sp", lamsb, lamv, w=["fs1"])
        S.dma("sp", cond_f[:], condT, w=["cond_f"])
        MSET(Vm[:, :, 1:8:3, :], 1.0, w=VM)
        MSET(Vd[:, :, 1:8:3, :], 1.0, w=VD)
        MSET(wuk[:], 0.0, w=["wuk"])
        MSET(Qp_u[:], 0.0, w=[("Qp_u", i_) for i_ in range(4)])
        MSET(Qp_r[:], 0.0, w=[("Qp_r", i_) for i_ in range(4)])
        MSET(epsb[:], EPS, w=["epsb"])
        for k_ in KM:
            S.alias("act", k_)
        PT.append(rope[:, 0:2, :])
        S.alias("PT2", "rope")
        xw2 = AB[:, :, :].bitcast(F32).rearrange("p a b -> p (a b)").rearrange("p (k n) -> p k n", k=8)
        hb2 = Kd[:, :, :].rearrange("p a b -> p (a b)")[:, 0:4096].rearrange("p (k n) -> p k n", k=8)
        for k_ in range(8):
            for b_ in range(16):
                S.alias(("xw2", k_), ("AB", b_))
            for c_ in range(3):
                S.alias(("hb2", k_), ("Kd", c_))
        BUFS = [(xw, hb, "xw", "hb"), (xw2, hb2, "xw2", "hb2")]
        slabs4 = [slabs[0], slabs[1],
                  Vm[:, :, :, :].rearrange("p a b c -> p (a b c)")[:, 0:SLABN],
                  Vd[:, :, :, :].rearrange("p a b c -> p (a b c)")[:, 0:SLABN]]
        for k_ in VM:
            S.alias(("slab", 2), k_)
        for k_ in VD:
            S.alias(("slab", 3), k_)

        ACT(cond_b[:], cond_f[:], AF.Silu, r=["cond_f"], w=["cond_b"])

        for l in range(n_layers):
            lam_init = 0.8 - 0.6 * math.exp(-0.3 * l)
            for i in range(2):
                TT(fs[0][:, i * 32:(i + 1) * 32], lamsb[:, l, 2 * i, :], lamsb[:, l, 2 * i + 1, :], ALU.mult,
                   r=["fs1"], w=["fs0"])
                RED(lamt[:, i:i + 1], fs[0][:, i * 32:(i + 1) * 32], r=["fs0"], w=["lamt"])
            ACT(lamt[:, 2:4], lamt[:, 0:2], AF.Exp, w=["lamt"])
            STT(neglam[:, l:l + 1], lamt[:, 3:4], -lam_init, lamt[:, 2:3], ALU.add, ALU.subtract,
                r=["lamt"], w=["neglam"])
            TSC(gv[:, l, 17:18], gv[:, l, 17:18], 1.0 - lam_init, ALU.mult, w=["gv"])

        def modulation(l, s6list, norms):
            b = bank("g")
            for s6 in s6list:
                si, slab, skey = next_slab()
                sv = slab[:, 0:8192].rearrange("p (k c) -> p k c", k=8)
                S.dma("sp", sv, wadab[l, s6], r=[("wadab", l, s6)], w=[skey])
                for m in range(8):
                    col = (s6 * 8 + m) * 2
                    for k in range(8):
                        mm(psum[:, b, col:col + 2], sv[:, k, m * 128:(m + 1) * 128], cond_b[:, k, :],
                           k == 0, k == 7, r=[skey, "cond_b"], w=[PS(b)])
            pv = psum[:, b, 0:96].rearrange("p (a c) -> p a c", c=2)
            lo, hi = min(s6list) * 8, (max(s6list) + 1) * 8
            for c in range(2):
                TT(modsb[:, l, lo:hi, c], pv[:, lo:hi, c], badaT[:, l, lo:hi], ALU.add, r=["badaT"], w=[PS(b), "modsb"])
            for nn, (stype, goff) in enumerate(((1, 0), (4, 8))):
                if nn not in norms:
                    continue
                for c in range(2):
                    STT(gsb[:, l, nn, :, c], modsb[:, l, stype * 8:(stype + 1) * 8, c], 1.0, gv[:, l, goff:goff + 8],
                        ALU.add, ALU.mult, r=["modsb", "gv"], w=["gsb"])

        pend_mod = []

        def modap(l, stype, k, c):
            return modsb[:, l, stype * 8 + k, c:c + 1]

        tcount = [0]

        def rstd_from_psum(b, n, inv_count):
            ACT(fs[3][:, 0:n], psum[:, b, 0:n], AF.Ln, w=[PS(b), "fs3"], scale=inv_count, bias=epsb[:, 0:1])
            ACT(fs[3][:, 0:n], fs[3][:, 0:n], AF.Exp, w=["fs3"], scale=-0.5)

        def norm_mod(l, n, nn, c, bs_=0):
            X, H, xk, hk = BUFS[bs_]
            for k in range(8):
                ACT(H[:, k, 0:n], X[:, k, 0:n], AF.Square, r=[(xk, k)], w=[(hk, k)])
            b = bank("g")
            for k in range(8):
                mm(psum[:, b, 0:n], mats[:, ONES_, :], H[:, k, 0:n], k == 0, k == 7, r=["mats", (hk, k)], w=[PS(b)])
            rstd_from_psum(b, n, 1.0 / D)
            for k in range(8):
                ti = tcount[0] % 2
                tcount[0] += 1
                tk = "fs%d" % ti
                TT(fs[ti][:, 0:n], X[:, k, 0:n], fs[3][:, 0:n], ALU.mult, r=[(xk, k), "fs3"], w=[tk])
                ACT(H[:, k, 0:n], fs[ti][:, 0:n], AF.Identity, r=[tk, "gsb", "modsb"], w=[(hk, k)],
                    scale=gsb[:, l, nn, k, c:c + 1], bias=modap(l, 0 if nn == 0 else 3, k, c))

        def load_x(xsrc, xkey, a, n, bs_=0):
            X, H, xk, hk = BUFS[bs_]
            blks = sorted(set([a // 512, (a + n - 1) // 512]))
            for hf in range(2):
                ks = range(4 * hf, 4 * hf + 4)
                S.dma("sp", X[:, 4 * hf:4 * hf + 4, 0:n], xsrc[:, 4 * hf:4 * hf + 4, a:a + n],
                      r=[(xkey, bb, k) for bb in blks for k in ks], w=[(xk, k) for k in ks])

        def store_x(xdst, xdkey, m0, va, vb, r0, r1, bs_=0):
            X, H, xk, hk = BUFS[bs_]
            blks = sorted(set([va // 512, (vb - 1) // 512]))
            ks = range(m0, m0 + 4)
            S.dma("sp", xdst[:, m0:m0 + 4, va:vb], X[:, m0:m0 + 4, r0:r1], r=[(xk, k) for k in ks],
                  w=[(xdkey, bb, k) for bb in blks for k in ks])

        def proj(pout, sv, c0, M, n, skey, b):
            for k in range(8):
                mm(pout, sv[:, k, c0:c0 + M], hb[:, k, 0:n], k == 0, k == 7, r=[skey, ("hb", k)], w=[PS(b)])

        def rms_rows(src_key, src_ap, n, ones_idx, inv_count, b2=None):
            if b2 is None:
                b2 = bank("g")
            ACT(bs[2][:, 0:n], src_ap, AF.Square, r=[src_key], w=["bs2"])
            mm(psum[:, b2, 0:n], mats[:, ones_idx, :], bs[2][:, 0:n], True, True, r=["mats", "bs2"], w=[PS(b2)])
            rstd_from_psum(b2, n, inv_count)

        def rope_apply(dst, dst_keys, src_b, src_key, nrow, perm_idx, ci, tcol0, n):
            b2 = bank("g")
            mm(psum[0:nrow, b2, 0:n], mats[0:nrow, perm_idx, 0:nrow], src_b, True, True, r=["mats", src_key], w=[PS(b2)])
            TT(fs[2][0:nrow, 0:n], psum[0:nrow, b2, 0:n], rope[0:nrow, ci + 1, 0:n], ALU.mult,
               r=["rope"], w=[PS(b2), "fs2"])
            TT(bs[1][0:nrow, 0:n], src_b, rope[0:nrow, ci, 0:n], ALU.mult, r=["rope", src_key], w=["bs1"],
               eng="dve")
            TT(dst, fs[2][0:nrow, 0:n], bs[1][0:nrow, 0:n], ALU.add, r=["fs2", "bs1"], w=dst_keys)

        def vdst(Vb, blk):
            return Vb[:, blk, :, :].rearrange("p (t g) b -> p t g b", g=3)[:, :, 0:3:2, :]

        def v6(ap384):
            return ap384.rearrange("p (t g b) -> p t g b", t=3, g=2)

        def build_km_vm(ckvn, ckvn_key, kr, kr_key, n, kcol0, kblk0):
            for h in range(6):
                b = bank("g")
                mm(psum[0:96, b, 0:n], wuk[:, h, :], ckvn, True, False, r=["wuk", ckvn_key], w=[PS(b)])
                mm(psum[0:96, b, 0:n], mats[0:32, SEL_, 0:96], kr, False, True, r=["mats", kr_key], w=[PS(b)])
                evac(Km[0:96, h, kcol0:kcol0 + n], psum[0:96, b, 0:n], w=[PS(b), ("Km", h)])
            for bi in range(n // 128):
                b = bank("g")
                mm(psum[:, b, 0:384], ckvn[:, bi * 128:(bi + 1) * 128], wuv[:], True, True, r=["wuv", ckvn_key], w=[PS(b)])
                evac(vdst(Vm, kblk0 + bi), v6(psum[:, b, 0:384]), w=[PS(b), ("Vm", kblk0 + bi)])

        stg_i = [0]

        def stg():
            i = stg_i[0] % 2
            stg_i[0] += 1
            return fs[i], "fs%d" % i

        def pass1_tile(l, strm, xsrc, xkey, col0, n, kcol0, kblk0, tcol0, skA):
            sv, skey = skA
            c = strm["cond"]
            lat = strm["rope"]
            load_x(xsrc, xkey, col0, n)
            if lat:
                S.dma("sp", rope[:, :, 0:n], ropeb[:, :, tcol0:tcol0 + n], r=["ropeb"], w=["rope"])
            norm_mod(l, n, 0, c)
            b = bank("g")
            proj(psum[:, b, 0:n], sv, 0, 128, n, skey, b)
            ACT(fs[2][:, 0:n], psum[:, b, 0:n], AF.Copy, w=[PS(b), "fs2"])
            rms_rows("fs2", fs[2][:, 0:n], n, ONES_, 1.0 / 128)
            TT(fs[2][:, 0:n], fs[2][:, 0:n], fs[3][:, 0:n], ALU.mult, r=["fs3"], w=["fs2"])
            ACT(bs[0][:, 0:n], fs[2][:, 0:n], AF.Copy, r=["fs2", "gv"], w=["bs0"], scale=gv[:, l, 16:17])
            if not lat:
                TSC(fs[2][:, 0:n], fs[2][:, 0:n], gv[:, l, 16:17], ALU.mult, r=["gv"], w=["fs2"])
                out_ops.append(S.dma("sp", ckv_o[l][:, col0:col0 + n], fs[2][:, 0:n], r=["fs2"]))
            b = bank("g")
            proj(psum[0:32, b, 0:n], sv, 128, 32, n, skey, b)
            if lat:
                ACT(bs[2][0:32, 0:n], psum[0:32, b, 0:n], AF.Copy, w=[PS(b), "bs2"])
                rope_apply(bs[2][0:32, 0:n], ["bs2"], bs[2][0:32, 0:n], "bs2", 32, PD_, 2, tcol0, n)
            else:
                sg, sgk = stg()
                ACT(sg[0:32, 0:n], psum[0:32, b, 0:n], AF.Copy, w=[PS(b), sgk])
                CPY(bs[2][0:32, 0:n], sg[0:32, 0:n], r=[sgk], w=["bs2"])
                out_ops.append(S.dma("sp", kr_o[l][:, col0:col0 + n], sg[0:32, 0:n], r=[sgk]))
            build_km_vm(bs[0][:, 0:n], "bs0", bs[2][0:32, 0:n], "bs2", n, kcol0, kblk0)
            prev_kd = None
            for cch in range(3):
                b = bank("g")
                proj(psum[:, b, 0:n], sv, 160 + 128 * cch, 128, n, skey, b)
                if lat:
                    sgi = 0 if cch % 2 == 0 else 2
                    ACT(bs[sgi][:, 0:n], psum[:, b, 0:n], AF.Copy, w=[PS(b), "bs%d" % sgi])
                    if prev_kd is not None:
                        rope_apply(*prev_kd)
                    prev_kd = (Kd[:, cch, kcol0:kcol0 + n], [("Kd", cch)], bs[sgi][:, 0:n], "bs%d" % sgi, 128, PD_, 2, tcol0, n)
                    if cch == 2:
                        rope_apply(*prev_kd)
                else:
                    sg, sgk = stg()
                    ACT(sg[:, 0:n], psum[:, b, 0:n], AF.Copy, w=[PS(b), sgk])
                    CPY(Kd[:, cch, kcol0:kcol0 + n], sg[:, 0:n], r=[sgk], w=[("Kd", cch)])
                    out_ops.append(S.dma("sp", kd_o[l][:, cch, col0:col0 + n], sg[:, 0:n], r=[sgk]))
            for bi in range(n // 128):
                b = bank("g")
                for k in range(8):
                    mm(psum[:, b, 0:384], hb[:, k, bi * 128:(bi + 1) * 128], sv[:, k, 544:928], k == 0, k == 7,
                       r=[skey, ("hb", k)], w=[PS(b)])
                if lat:
                    evac(vdst(Vd, kblk0 + bi), v6(psum[:, b, 0:384]), w=[PS(b), ("Vd", kblk0 + bi)])
                else:
                    sg, sgk = stg()
                    ACT(sg[:, 0:384], psum[:, b, 0:384], AF.Copy, w=[PS(b), sgk])
                    CPY(vdst(Vd, kblk0 + bi), v6(sg[:, 0:384]), r=[sgk], w=[("Vd", kblk0 + bi)])
                    out_ops.append(S.dma("sp", vd_o[l][col0 + bi * 128:col0 + (bi + 1) * 128, :], sg[:, 0:384], r=[sgk]))
            for j in range(2):
                b = bank("g")
                proj(psum[:, b, 0:n], sv, 928 + 128 * j, 128, n, skey, b)
                evac(bs[j][:, 0:n], psum[:, b, 0:n], w=[PS(b), "bs%d" % j])
            for bi in range(n // 128):
                b = bank("g")
                for j in range(2):
                    mm(psum[:, b, 0:512], bs[j][:, bi * 128:(bi + 1) * 128], dch[:, j, :], j == 0, j == 1,
                       r=["PT0", "bs%d" % j], w=[PS(b)])
                evac(AB[:, kblk0 + bi, :], psum[:, b, 0:512], w=[PS(b), ("AB", kblk0 + bi)])

        pending = []

        def flush_pending():
            while pending:
                pending.pop(0)()

        def attn_job(maps, entries, scale):
            accb = [bank("acc"), bank("acc")]
            nkb = len(entries)

            def av(i):
                kb, qc0, nq, rotf, st_, sp_ = entries[i]
                pt = PT[i % 3]
                ptk = "PT%d" % (i % 3)
                for mi, mp in enumerate(maps):
                    mm(psum[:, accb[mi], qc0:qc0 + nq], mp["V"](kb), pt[:, mi, 0:nq], st_, sp_,
                       r=[ptk, (mp["vkey"], kb)], w=[PS(accb[mi])])

            sbs = {}

            def scores(i):
                kb, qc0, nq, rotf, st_, sp_ = entries[i]
                sb0 = bank("s")
                sbs[i] = sb0
                for mi, mp in enumerate(maps):
                    lhsT, tp = mp["K"](kb)
                    mm(psum[:, sb0 + mi, 0:nq], lhsT, mp["Q"](kb, qc0, nq, rotf), True, True, r=mp["r"], w=[PS(sb0 + mi)], tp=tp)

            scores(0)
            for i, (kb, qc0, nq, rotf, st_, sp_) in enumerate(entries):
                if i + 1 < nkb:
                    scores(i + 1)
                sb0 = sbs[i]
                pt = PT[i % 3]
                ptk = "PT%d" % (i % 3)
                ACT(pt[:, :, 0:nq], psum[:, sb0:sb0 + 2, 0:nq], AF.Exp, w=[PS(sb0), PS(sb0 + 1), ptk], scale=scale)
                if i >= 1:
                    av(i - 1)
                if i == min(3, nkb - 1):
                    flush_pending()
            av(nkb - 1)
            return accb

        def vsel(Vb, h, kb):
            hp_ = h // 2
            c0_ = 192 * hp_ + (64 if h % 2 else 0)
            return Vb[:, kb, :, :].rearrange("p a b -> p (a b)")[:, c0_:c0_ + 128]

        def mixer_pass2(l, strm, xsrc, xkey, xdst, xdkey, col0, nq, units, tcol0, skQ, fourier_fn):
            c = strm["cond"]
            lat = strm["rope"]
            sv, skey = skQ
            load_x(xsrc, xkey, col0, nq)
            if lat:
                S.dma("sp", rope[:, :, 0:nq], ropeb[:, :, tcol0:tcol0 + nq], r=["ropeb"], w=["rope"])
            norm_mod(l, nq, 0, c)
            prev_rope = None
            for h in range(6):
                b = bank("g")
                proj(psum[0:96, b, 0:nq], sv, 96 * h, 96, nq, skey, b)
                ACT(Qm_u[0:96, h, 0:nq], psum[0:96, b, 0:nq], AF.Copy, w=[PS(b), ("Qm_u", h)])
                if lat:
                    if prev_rope is not None:
                        rope_apply(*prev_rope)
                    prev_rope = (Qm_r[0:96, h, 0:nq], [("Qm_r", h)], Qm_u[0:96, h, 0:nq], ("Qm_u", h), 96, PM_, 0, tcol0, nq)
            for cch in range(3):
                b = bank("g")
                proj(psum[:, b, 0:nq], sv, 576 + 128 * cch, 128, nq, skey, b)
                ACT(Qd_u[:, cch, 0:nq], psum[:, b, 0:nq], AF.Copy, w=[PS(b), ("Qd_u", cch)])
                if lat:
                    if prev_rope is not None:
                        rope_apply(*prev_rope)
                    prev_rope = (Qd_r[:, cch, 0:nq], [("Qd_r", cch)], Qd_u[:, cch, 0:nq], ("Qd_u", cch), 128, PD_, 2, tcol0, nq)
            if lat and prev_rope is not None:
                rope_apply(*prev_rope)

            rcp_mode["mix"] = not lat
            for entries in units:
                kbs = entries
                qc0, qn = 0, nq
                for m_ in range(2):
                    PCPY(Qp_u[m_ * 32:m_ * 32 + 32, m_, 0:nq], Qd_u[m_ * 32:m_ * 32 + 32, 0, 0:nq], r=[("Qd_u", 0)], w=[("Qp_u", m_)])
                    if lat:
                        PCPY(Qp_r[m_ * 32:m_ * 32 + 32, m_, 0:nq], Qd_r[m_ * 32:m_ * 32 + 32, 0, 0:nq], r=[("Qd_r", 0)], w=[("Qp_r", m_)])
                for hp in range(3):
                    maps = []
                    for h in (2 * hp, 2 * hp + 1):
                        maps.append({
                            "K": (lambda kb, h=h: (Km[0:96, h, kb * 128:(kb + 1) * 128], None)),
                            "Q": (lambda kb, c0, n_, rf, h=h: (Qm_r if rf else Qm_u)[0:96, h, c0:c0 + n_]),
                            "V": (lambda kb, h=h: vsel(Vm, h, kb)),
                            "r": [("Km", h), ("Qm_u", h), ("Qm_r", h)], "vkey": "Vm"})
                    accb = attn_job(maps, kbs, MLA_SCALE)
                    for mi, h in enumerate((2 * hp, 2 * hp + 1)):
                        a = accb[mi]
                        N0 = (h % 2) * 64
                        D0 = 64 - N0
                        RCP(fs[2][D0:D0 + 64, 0:qn], psum[D0:D0 + 64, a, 0:qn], w=[PS(a), "fs2"])
                        TT(oT[N0:N0 + 64, hp, qc0:qc0 + qn], psum[N0:N0 + 64, a, 0:qn],
                           fs[2][D0:D0 + 64, 0:qn], ALU.mult, r=["fs2"], w=[PS(a), ("hb", hp)])
                djobs = [(cch_, hh_) for cch_ in range(3) for hh_ in range(2)]

                def pad_copies(cch_, hh_):
                    for m in range(2):
                        r0 = hh_ * 64 + m * 32
                        slot = hh_ * 2 + m
                        PCPY(Qp_u[r0:r0 + 32, slot, 0:nq], Qd_u[r0:r0 + 32, cch_, 0:nq], r=[("Qd_u", cch_)], w=[("Qp_u", slot)])
                        if lat:
                            PCPY(Qp_r[r0:r0 + 32, slot, 0:nq], Qd_r[r0:r0 + 32, cch_, 0:nq], r=[("Qd_r", cch_)], w=[("Qp_r", slot)])

                for cch in range(3):
                    for hh in range(2):
                        h = 2 * cch + hh
                        ji = 2 * cch + hh
                        if ji + 1 < len(djobs):
                            pad_copies(*djobs[ji + 1])
                        maps = []
                        for m in range(2):
                            slot = hh * 2 + m
                            maps.append({
                                "K": (lambda kb, cch=cch: (Kd[:, cch, kb * 128:(kb + 1) * 128], None)),
                                "Q": (lambda kb, c0, n_, rf, slot=slot: (Qp_r if rf else Qp_u)[:, slot, c0:c0 + n_]),
                                "V": (lambda kb, h=h: vsel(Vd, h, kb)),
                                "r": [("Kd", cch), ("Qp_u", slot), ("Qp_r", slot)], "vkey": "Vd"})
                        accb = attn_job(maps, kbs, DIFF_SCALE)
                        a1, a2 = accb
                        N0 = (h % 2) * 64
                        N1 = N0 + 64
                        D0 = 64 - N0
                        D1 = D0 + 64
                        RCP(fs[2][D0:D1, 0:qn], psum[D0:D1, a1, 0:qn], w=[PS(a1), "fs2"])
                        TT(fs[0][N0:N1, 0:qn], psum[N0:N1, a1, 0:qn], fs[2][D0:D1, 0:qn], ALU.mult, r=["fs2"], w=[PS(a1), "fs0"])
                        RCP(fs[2][D0:D1, 0:qn], psum[D0:D1, a2, 0:qn], w=[PS(a2), "fs2"])
                        TT(fs[2][N0:N1, 0:qn], psum[N0:N1, a2, 0:qn], fs[2][D0:D1, 0:qn], ALU.mult, w=[PS(a2), "fs2"])
                        STT(fs[1][N0:N1, 0:qn], fs[2][N0:N1, 0:qn], neglam[N0:N1, l:l + 1], fs[0][N0:N1, 0:qn],
                            ALU.mult, ALU.add, r=["fs2", "fs0", "neglam"], w=["fs1"])
                    ACT(bs[2][:, 0:qn], fs[1][:, 0:qn], AF.Square, r=["fs1"], w=["bs2"])

                    def post(cch=cch, a2=a2, qc0=qc0, qn=qn):
                        mm(psum[:, a2, 0:qn], mats[:, BD64_, :], bs[2][:, 0:qn], True, True, r=["mats", "bs2"], w=[PS(a2)])
                        rstd_from_psum(a2, qn, 1.0 / 64)
                        TT(fs[1][:, 0:qn], fs[1][:, 0:qn], fs[3][:, 0:qn], ALU.mult, r=["fs3"], w=["fs1"])
                        ACT(oT[:, 3 + cch, qc0:qc0 + qn], fs[1][:, 0:qn], AF.Copy, r=["fs1", "gv"], w=[("hb", 3 + cch)],
                            scale=gv[:, l, 17:18])
                    pending.append(post)
            flush_pending()
            fourier_fn()
            while pend_mod:
                pend_mod.pop(0)()
            si, slab, sk = next_slab()
            svo = slab[:, 0:8192].rearrange("p (k c) -> p k c", k=8)
            S.dma("sp", svo, woutb[l], r=[("woutb", l)], w=[sk])
            for m in range(8):
                b = bank("g")
                for k in range(8):
                    mm(psum[:, b, 0:nq], svo[:, k, m * 128:(m + 1) * 128], oT[:, k, 0:nq], k == 0, k == 7,
                       r=[sk, ("hb", k)], w=[PS(b)])
                STT(xw[:, m, 0:nq], psum[:, b, 0:nq], modap(l, 2, m, c), xw[:, m, 0:nq], ALU.mult, ALU.add,
                    r=["modsb"], w=[PS(b), ("xw", m)])
                if m in (3, 7):
                    store_x(xdst, xdkey, m - 3, col0, col0 + nq, 0, nq)

        def ffn_phase(l, strm, xsrc, xkey, xdst, xdkey, windows, last):
            c = strm["cond"]
            steps = []
            for wi in range(len(windows)):
                for g in range(6):
                    steps.append(("U", wi, g))
                for mp in range(4):
                    steps.append(("D", wi, mp))
            sinfo = {}

            def issue_load(si):
                kind, wi, idx = steps[si]
                _, slab, sk = next_slab()
                if kind == "U":
                    j0, nj = FGROUPS[idx]
                    sv = slab[:, 0:8192].rearrange("p (k c) -> p k c", k=8)
                    if nj == 4:
                        S.dma("sp", sv, wupb[l, idx], r=[("wupb", l, idx)], w=[sk])
                    else:
                        S.dma("sp", sv[:, :, 0:128 * nj], wupb[l, idx][:, :, 0:128 * nj], r=[("wupb", l, idx)], w=[sk])
                        S.dma("sp", sv[:, :, 512:512 + 128 * nj], wupb[l, idx][:, :, 512:512 + 128 * nj],
                              r=[("wupb", l, idx)], w=[sk])
                else:
                    sv = slab[:, 0:22 * 256].rearrange("p (j c) -> p j c", j=22)
                    S.dma("sp", sv, wdnb[l, idx], r=[("wdnb", l, idx)], w=[sk])
                sinfo[si] = (sv, sk)

            nxt = 0
            while nxt < min(4, len(steps)):
                issue_load(nxt)
                nxt += 1
            a0, n0 = windows[0][0], windows[0][1]
            load_x(xsrc, xkey, a0, n0, 0)
            norm_mod(l, n0, 1, c, 0)
            for si, (kind, wi, idx) in enumerate(steps):
                a, n, va, vb, segs = windows[wi]
                bset = wi % 2
                X, H, xk, hk = BUFS[bset]
                sv, sk = sinfo[si]
                if kind == "U":
                    j0, nj = FGROUPS[idx]
                    for jj in range(nj):
                        j = j0 + jj
                        for part in range(2):
                            b = bank("g")
                            ch = j + 22 * part
                            for k in range(8):
                                mm(psum[:, b, 0:n], sv[:, k, part * 512 + jj * 128:part * 512 + (jj + 1) * 128], H[:, k, 0:n],
                                   k == 0, k == 7, r=[sk, (hk, k)], w=[PS(b)])
                            cb = fs[part]
                            ck = "fs%d" % part
                            ACT(cb[:, 0:n], psum[:, b, 0:n], AF.Copy, r=["wcv"], w=[PS(b), ck], scale=wcv1[:, 1, ch:ch + 1])
                            nseg = len(segs)
                            Ls = n // nseg
                            cb3 = cb[:, 0:n].rearrange("p (s m) -> p s m", s=nseg)
                            pu3 = psum[:, b, 0:n].rearrange("p (s m) -> p s m", s=nseg)
                            STT(cb3[:, :, 1:Ls], pu3[:, :, 0:Ls - 1], wcv1[:, 0, ch:ch + 1], cb3[:, :, 1:Ls],
                                ALU.mult, ALU.add, r=["wcv"], w=[PS(b), ck])
                            STT(cb3[:, :, 0:Ls - 1], pu3[:, :, 1:Ls], wcv1[:, 2, ch:ch + 1], cb3[:, :, 0:Ls - 1],
                                ALU.mult, ALU.add, r=["wcv"], w=[PS(b), ck])
                        ACT(fs[2][:, 0:n], fs[0][:, 0:n], AF.Silu, r=["fs0"], w=["fs2"])
                        TT(act[:, j, 0:n], fs[2][:, 0:n], fs[1][:, 0:n], ALU.mult, r=["fs2", "fs1"], w=["act"])
                else:
                    for mm_i in range(2):
                        m = 2 * idx + mm_i
                        b = bank("g")
                        for j in range(22):
                            mm(psum[:, b, 0:n], sv[:, j, mm_i * 128:(mm_i + 1) * 128], act[:, j, 0:n], j == 0, j == 21,
                               r=[sk, "act"], w=[PS(b)])
                        STT(X[:, m, 0:n], psum[:, b, 0:n], modap(l, 5, m, c), X[:, m, 0:n], ALU.mult, ALU.add,
                            r=["modsb"], w=[PS(b), (xk, m)])
                if nxt < len(steps):
                    issue_load(nxt)
                    nxt += 1
                if wi + 1 < len(windows):
                    an, nn_ = windows[wi + 1][0], windows[wi + 1][1]
                    if kind == "U" and idx == 4:
                        load_x(xsrc, xkey, an, nn_, 1 - bset)
                    if kind == "D" and idx == 0:
                        norm_mod(l, nn_, 1, c, 1 - bset)
                if kind == "D" and not last and idx in (1, 3):
                    store_x(xdst, xdkey, (idx // 2) * 4, va, vb, va - a, vb - a, bset)
                if kind == "D" and idx == 3 and last:
                    r0, r1 = va - a, vb - a
                    b = bank("g")
                    for k in range(8):
                        ACT(bs[k % 2][:, 0:n], X[:, k, 0:n], AF.Square, r=[(xk, k)], w=["bs%d" % (k % 2)])
                        mm(psum[:, b, 0:n], mats[:, ONES_, :], bs[k % 2][:, 0:n], k == 0, k == 7, r=["mats", "bs%d" % (k % 2)], w=[PS(b)])
                    rstd_from_psum(b, n, 1.0 / D)
                    for k in range(8):
                        STT(X[:, k, 0:n], X[:, k, 0:n], gfin[:, k:k + 1], fs[3][:, 0:n], ALU.mult, ALU.mult,
                            r=["fs3", "gfin"], w=[(xk, k)])
                    for hf in range(2):
                        out_ops.append(S.dma("sp", yT[:, 4 * hf:4 * hf + 4, va:vb], X[:, 4 * hf:4 * hf + 4, r0:r1],
                                             r=[(xk, k) for k in range(4 * hf, 4 * hf + 4)]))

        STR_C = {"name": "c", "cond": 0, "rope": False}
        STR_L = {"name": "l", "cond": 1, "rope": True}

        def load_win_slab(l, which):
            si, slab, sk = next_slab()
            sv = slab[:, 0:8 * 1184].rearrange("p (k c) -> p k c", k=8)
            if which == "kv":
                S.dma("sp", sv, winb[l, 0], r=[("winb", l, 0)], w=[sk])
            else:
                S.dma("sp", sv[:, :, 0:960], winb[l, 1][:, :, 0:960], r=[("winb", l, 1)], w=[sk])
            return sv, sk

        def make_fourier(lat, t):
            if lat:
                def fourier_fn():
                    accs = [bank("g"), bank("g")]
                    for hh in range(2):
                        si, slab, sk = next_slab()
                        sv = slab[:, 0:8192].rearrange("p (b c n) -> p b c n", b=8, c=2)
                        S.dma("sp", slab[:, 0:8192], dftlb[t, hh], r=[("dftlb", t, hh)], w=[sk])
                        for j in range(2):
                            for bb in range(8):
                                blk = hh * 8 + bb
                                for part in range(2):
                                    first = (hh == 0 and bb == 0 and part == 0)
                                    lastm = (hh == 1 and bb == 7 and part == 1)
                                    mm(psum[:, accs[j], 0:512], AB[:, blk, part * 256 + j * 128:part * 256 + (j + 1) * 128],
                                       sv[:, bb, part, :], first, lastm, r=[("AB", blk), sk], w=[PS(accs[j])])
                    for j in range(2):
                        evac(oT[:, 6 + j, 0:512], psum[:, accs[j], 0:512], w=[PS(accs[j]), ("hb", 6 + j)])
            else:
                def fourier_fn():
                    si, slab, sk = next_slab()
                    sv = slab[:, 0:1024].rearrange("p (b c n) -> p b c n", b=2, c=2)
                    S.dma("sp", slab[:, 0:1024], dftcb, r=["dftcb"], w=[sk])
                    for s in range(2):
                        for j in range(2):
                            b = bank("g")
                            for bb in range(2):
                                blk = 4 * t + 2 * s + bb
                                for part in range(2):
                                    mm(psum[:, b, 0:256], AB[:, blk, part * 256 + j * 128:part * 256 + (j + 1) * 128],
                                       sv[:, bb, part, :], bb == 0 and part == 0, bb == 1 and part == 1,
                                       r=[("AB", blk), sk], w=[PS(b)])
                            evac(oT[:, 6 + j, 256 * s:256 * (s + 1)], psum[:, b, 0:256], w=[PS(b), ("hb", 6 + j)])
            return fourier_fn

        for l in range(n_layers):
            last = (l == n_layers - 1)
            xsrc = xT if l == 0 else xs[0]
            xkey = "xT" if l == 0 else "xs0"
            xmid, xmkey = xs[1], "xs1"
            xdst, xdkey = xs[0], "xs0"
            if l == 0:
                modulation(l, [0, 1], [0])
                pend_mod.append(lambda l=l: modulation(l, [2, 3, 4, 5], [1]))
            else:
                modulation(l, [0, 1, 2, 3, 4, 5], [0, 1])
            S.dma("sp", wcv1[:], w_convT[:, l], w=["wcv"])
            S.dma("sp", wuk[:, :, 0:64], wukb[l].rearrange("p (h c) -> p h c", h=6), r=[("wukb", l)], w=["wuk"])
            S.dma("sp", wuv[:], wuvb[l], r=[("wuvb", l)], w=["wuv"])
            for strm in (STR_C, STR_L):
                lat = strm["rope"]
                phase["ffn"] = False
                base = NCTX if lat else 0
                ntile = 4 if lat else 2
                MSET(Vm[:, :, 1:8:3, :], 1.0, w=VM)
                MSET(Vd[:, :, 1:8:3, :], 1.0, w=VD)
                S.dma("sp", PT[0][:], dchb, r=["dchb"], w=["PT0"])
                skA = load_win_slab(l, "kv")
                for t in range(ntile):
                    pass1_tile(l, strm, xsrc, xkey, base + 512 * t, 512, 512 * t, 4 * t, 512 * t, skA)
                if lat:
                    S.dma("sp", bs[0][:, 0:512], ckvcb[l], r=[("cache", l)], w=["bs0"])
                    S.dma("sp", bs[2][0:32, 0:512], krcb[l], r=[("cache", l)], w=["bs2"])
                    build_km_vm(bs[0][:, 0:512], "bs0", bs[2][0:32, 0:512], "bs2", 512, 2048, 16)
                    S.dma("sp", Kd[:, :, 2048:2560], kdcb[l], r=[("cache", l)], w=KD)
                    for bq in range(4):
                        for t3 in range(3):
                            S.dma("sp", Vd[:, 16 + bq, 3 * t3:3 * t3 + 3:2, :],
                                  vdcb[l][:, bq, 128 * t3:128 * t3 + 128].rearrange("p (g b) -> p g b", g=2),
                                  r=[("cache", l)], w=[("Vd", 16 + bq)])
                for t in range(ntile):
                    col0 = base + 512 * t
                    skQ = load_win_slab(l, "q")
                    if lat:
                        units = [[(kb, 0, 512, kb < 16, kb == 0, kb == 19) for kb in range(20)]]
                    else:
                        units = [[(4 * t + 2 * s + bb, 256 * s, 256, False, bb == 0, bb == 1)
                                  for s in range(2) for bb in range(2)]]
                    mixer_pass2(l, strm, xsrc, xkey, xmid, xmkey, col0, 512, units, 512 * t, skQ, make_fourier(lat, t))
                phase["ffn"] = True
                if lat:
                    bnds = [0, 410, 820, 1230, 1639, 2048]
                    wins = []
                    for wi in range(5):
                        a_ = max(0, bnds[wi] - 1)
                        e_ = min(NLAT, bnds[wi + 1] + 1)
                        wins.append((base + a_, e_ - a_, base + bnds[wi], base + bnds[wi + 1], [(0, e_ - a_)]))
                else:
                    wins = [(512 * t, 512, 512 * t, 512 * t + 512, [(0, 256), (256, 512)]) for t in range(2)]
                ffn_phase(l, strm, xmid, xmkey, xdst, xdkey, wins, last)
        S.emit(out_ops)
        build_program.stats = (S.nops, {e: len(S.prog[e]) for e in S.ENGS})
    return nc


def _prep_inputs(inp):
    if "c" not in _CONST_CACHE:
        _CONST_CACHE["c"] = _consts()
    C = _CONST_CACHE["c"]
    f = lambda a: np.ascontiguousarray(np.asarray(a, dtype=np.float32))
    x_prompt = f(inp["x_prompt"])
    x_sample = f(inp["x_sample"])
    cvec = f(inp["c"])
    c_ctx = f(inp["c_ctx"])

    def fm(v):
        v = np.asarray(v)
        lead = v.shape[:-1]
        vv = v.reshape(lead + (8, 128))
        return np.ascontiguousarray(np.moveaxis(np.moveaxis(vv, -1, 0), -1, 1))

    shared = {}
    shared["w_ada"] = f(inp["w_ada"])
    shared["b_adaT"] = np.ascontiguousarray(f(inp["b_ada"]).reshape(DEPTH, 48, 128).transpose(2, 0, 1))
    gvec = np.zeros((128, DEPTH, 18), np.float32)
    gvec[:, :, 0:8] = f(inp["g_mix_norm"]).reshape(DEPTH, 8, 128).transpose(2, 0, 1)
    gvec[:, :, 8:16] = f(inp["g_ffn_norm"]).reshape(DEPTH, 8, 128).transpose(2, 0, 1)
    gvec[:, :, 16] = f(inp["g_kv_norm"]).T
    shared["gvec_raw_gsub"] = np.ascontiguousarray(np.tile(f(inp["g_diff_subln"]).T, (2, 1)))
    shared["gvec"] = gvec
    shared["g_finalT"] = np.ascontiguousarray(f(inp["g_final"]).reshape(8, 128).T)
    shared["w_in"] = f(inp["w_in"])
    shared["w_uk"] = f(inp["w_uk"])
    shared["w_uv"] = f(inp["w_uv"])
    lam = np.stack([f(inp["lam_q1"]), f(inp["lam_k1"]), f(inp["lam_q2"]), f(inp["lam_k2"])], axis=1)
    shared["lamv"] = np.ascontiguousarray(np.broadcast_to(lam[None], (128, DEPTH, 4, 32)))
    shared["w_out"] = f(inp["w_out"])
    shared["w_up"] = f(inp["w_up"])
    shared["w_convT"] = np.ascontiguousarray(f(inp["w_conv"]).reshape(DEPTH, 3, 44, 128).transpose(3, 0, 1, 2))
    shared["w_down"] = f(inp["w_down"])
    shared["c_rope"] = C["rope"]
    shared["c_mats"] = C["mats"]
    shared["c_dch"] = C["dch"]
    shared["c_dftc"] = C["dftc"]
    shared["c_dftl"] = C["dftl"]
    cm = f(inp["cache_mla_ckv"])
    ck = f(inp["cache_mla_krope"])
    cdk = f(inp["cache_diff_k"])
    cdv = f(inp["cache_diff_v"])
    in_maps = []
    for i in range(8):
        m = dict(shared)
        xt = np.concatenate([x_prompt[4 * i:4 * i + 4].reshape(NCTX, D), x_sample[i]], axis=0)
        m["xT"] = np.ascontiguousarray(xt.reshape(NTOK, 8, 128).transpose(2, 1, 0))
        cond = np.stack([c_ctx, cvec[i]], axis=-1)
        m["condT"] = np.ascontiguousarray(cond.reshape(8, 128, 2).transpose(1, 0, 2))
        m["ckvcT"] = np.ascontiguousarray(cm[i].transpose(0, 2, 1))
        m["krcT"] = np.ascontiguousarray(ck[i].transpose(0, 2, 1))
        kd = cdk[i].reshape(DEPTH, PAST, 3, 128)
        m["kdcT"] = np.ascontiguousarray(kd.transpose(0, 3, 2, 1))
        vd = cdv[i].reshape(DEPTH, 4, 128, 384)
        m["vdc"] = np.ascontiguousarray(vd.transpose(0, 2, 1, 3))
        in_maps.append(m)
    return in_maps


_NC_CACHE = {}


def kernel(**inputs):
    return _run(inputs, DEPTH)


def _run(inputs, n_layers):
    in_maps = _prep_inputs(inputs)
    for m in in_maps:
        gs = m.pop("gvec_raw_gsub")
        gvec = m["gvec"]
        gvec[:, :, 17] = gs
    if n_layers not in _NC_CACHE:
        _NC_CACHE[n_layers] = build_program(n_layers)
    nc = _NC_CACHE[n_layers]
    res = run_bass_kernel_spmd(nc, in_maps, core_ids=list(range(8)))
    R = res.results
    y_prompt = np.zeros((32, 256, D), np.float32)
    y_sample = np.zeros((8, 2048, D), np.float32)
    n_ckv = np.zeros((32, DEPTH, 256, 128), np.float32)
    n_kr = np.zeros((32, DEPTH, 256, 32), np.float32)
    n_dk = np.zeros((32, DEPTH, 256, 6, 64), np.float32)
    n_dv = np.zeros((32, DEPTH, 256, 6, 64), np.float32)
    for i in range(8):
        r = R[i]
        y = np.asarray(r["yT"]).transpose(2, 1, 0).reshape(NTOK, D)
        y_prompt[4 * i:4 * i + 4] = y[:NCTX].reshape(4, 256, D)
        y_sample[i] = y[NCTX:]
        ck = np.asarray(r["ckv_o"])
        n_ckv[4 * i:4 * i + 4] = ck.reshape(DEPTH, 128, 4, 256).transpose(2, 0, 3, 1)
        kr = np.asarray(r["kr_o"])
        n_kr[4 * i:4 * i + 4] = kr.reshape(DEPTH, 32, 4, 256).transpose(2, 0, 3, 1)
        kd = np.asarray(r["kd_o"])
        n_dk[4 * i:4 * i + 4] = kd.reshape(DEPTH, 128, 3, 4, 256).transpose(3, 0, 4, 2, 1).reshape(4, DEPTH, 256, 6, 64)
        vd = np.asarray(r["vd_o"])
        n_dv[4 * i:4 * i + 4] = vd.reshape(DEPTH, 4, 256, 6, 64).transpose(1, 0, 2, 3, 4)
    return (y_prompt, y_sample, n_ckv, n_kr, n_dk, n_dv)
```
